# Optimizing a Trainium2 kernel written in Bass

```python
import math
import jax, jax.numpy as jnp
from jax import lax
import numpy as np

D_MODEL = 1024
BATCH = 4
SEQ = 8192
DEPTH = 4

HEAD_DIM = 64
A_HEADS = 4
B_Q_HEADS = 8
B_KV_HEADS = 2
C_HEADS = 16
ROT_DIM = HEAD_DIM // 4
ROPE_THETA = 500000.0
AXIAL_THETA = 10000.0
GRID_W = 64
NA_KH = 8
NA_KW = 16
Q_BLOCK = 128
D_FF = 2816
CONV_W = 3
PLE_DIM = 256
EPS = 1e-6

N_EVEN = (DEPTH + 1) // 2
N_ODD = DEPTH // 2
A_QK = A_HEADS * 2 * HEAD_DIM
A_V = A_HEADS * 2 * HEAD_DIM
B_Q = B_Q_HEADS * HEAD_DIM
B_KV = B_KV_HEADS * HEAD_DIM
EVEN_IN = 2 * A_QK + A_V + B_Q + 2 * B_KV
EVEN_SPLITS = [A_QK, 2 * A_QK, 2 * A_QK + A_V, 2 * A_QK + A_V + B_Q, 2 * A_QK + A_V + B_Q + B_KV]
EVEN_MIX = A_V + B_Q
C_IN = 3 * C_HEADS * HEAD_DIM
C_MIX = C_HEADS * HEAD_DIM

kernel_name = "hybrid_diff_gqa_axial_natten_convffn_ple"


def rms_norm(x, g):
    xf = x.astype(jnp.float32)
    y = xf * lax.rsqrt(jnp.mean(xf * xf, axis=-1, keepdims=True) + EPS)
    return (y * g.astype(jnp.float32)).astype(x.dtype)


def rotary(x, pos, theta):
    d = x.shape[-1]
    half = d // 2
    inv = theta ** (-jnp.arange(half, dtype=jnp.float32) / half)
    ang = pos.astype(jnp.float32)[:, None] * inv[None, :]
    shp = (1, x.shape[1]) + (1,) * (x.ndim - 3) + (half,)
    cos = jnp.cos(ang).reshape(shp)
    sin = jnp.sin(ang).reshape(shp)
    xf = x.astype(jnp.float32)
    x1, x2 = xf[..., :half], xf[..., half:]
    return jnp.concatenate([x1 * cos - x2 * sin, x2 * cos + x1 * sin], axis=-1).astype(x.dtype)


def partial_rope(x, pos):
    return jnp.concatenate([rotary(x[..., :ROT_DIM], pos, ROPE_THETA), x[..., ROT_DIM:]], axis=-1)


def axial_rope(x, row, col):
    half = x.shape[-1] // 2
    return jnp.concatenate([rotary(x[..., :half], row, AXIAL_THETA),
                            rotary(x[..., half:], col, AXIAL_THETA)], axis=-1)


def to_blocks(t):
    b, s = t.shape[:2]
    return jnp.moveaxis(t.reshape((b, s // Q_BLOCK, Q_BLOCK) + t.shape[2:]), 1, 0)


def from_blocks(t):
    t = jnp.moveaxis(t, 0, 1)
    return t.reshape((t.shape[0], t.shape[1] * t.shape[2]) + t.shape[3:])


def diff_attention(q, k, v, lam):
    scale = HEAD_DIM ** -0.5

    def block(qb):
        s = jnp.einsum('bqhjd,bkhjd->bhjqk', qb, k).astype(jnp.float32) * scale
        pr = jax.nn.softmax(s, axis=-1)
        w = (pr[:, :, 0] - lam * pr[:, :, 1]).astype(v.dtype)
        return jnp.einsum('bhqk,bkhe->bqhe', w, v)

    return from_blocks(lax.map(block, to_blocks(q)))


def gqa_attention(q, k, v):
    scale = HEAD_DIM ** -0.5

    def block(qb):
        s = jnp.einsum('bqgrd,bkgd->bgrqk', qb, k).astype(jnp.float32) * scale
        pr = jax.nn.softmax(s, axis=-1).astype(v.dtype)
        return jnp.einsum('bgrqk,bkgd->bqgrd', pr, v)

    return from_blocks(lax.map(block, to_blocks(q)))


def neighbourhood_attention(q, k, v, bias_table):
    b, s, h, d = q.shape
    rows = s // GRID_W
    kh = min(NA_KH, rows)
    kw = NA_KW
    scale = d ** -0.5
    qg = q.reshape(b, rows, GRID_W, h, d)
    kg = k.reshape(b, rows, GRID_W, h, d)
    vg = v.reshape(b, rows, GRID_W, h, d)
    r_idx = jnp.arange(rows, dtype=jnp.int32)
    row_start = jnp.clip(r_idx - kh // 2, 0, rows - kh)
    cols = jnp.arange(GRID_W, dtype=jnp.int32)
    col_start = jnp.clip(cols - kw // 2, 0, GRID_W - kw)
    col_valid = (cols[None, :] >= col_start[:, None]) & (cols[None, :] < col_start[:, None] + kw)
    dc_idx = jnp.clip(cols[None, :] - cols[:, None] + NA_KW - 1, 0, 2 * NA_KW - 2)

    def row_block(args):
        r, rs, qr = args
        kr = lax.dynamic_slice_in_dim(kg, rs, kh, axis=1)
        vr = lax.dynamic_slice_in_dim(vg, rs, kh, axis=1)
        sc = jnp.einsum('bqhd,bjkhd->bhqjk', qr, kr).astype(jnp.float32) * scale
        dr_idx = rs + jnp.arange(kh, dtype=jnp.int32) - r + NA_KH - 1
        bias = bias_table[:, dr_idx[None, :, None], dc_idx[:, None, :]]
        sc = sc + bias.astype(jnp.float32)[None]
        sc = jnp.where(col_valid[:, None, :], sc, -jnp.inf)
        pr = jax.nn.softmax(sc, axis=(-2, -1)).astype(vr.dtype)
        return jnp.einsum('bhqjk,bjkhd->bqhd', pr, vr)

    out = lax.map(row_block, (r_idx, row_start, jnp.moveaxis(qg, 1, 0)))
    return jnp.moveaxis(out, 0, 1).reshape(b, s, h * d)


def even_mixer(h, w_in, lq1, lk1, lq2, lk2, subln, q_norm, k_norm, w_out, lam_init, pos, row, col):
    b, s, _ = h.shape
    a_q, a_k, a_v, b_q, b_k, b_v = jnp.split(h @ w_in, EVEN_SPLITS, axis=-1)
    a_q = partial_rope(a_q.reshape(b, s, 2 * A_HEADS, HEAD_DIM), pos).reshape(b, s, A_HEADS, 2, HEAD_DIM)
    a_k = partial_rope(a_k.reshape(b, s, 2 * A_HEADS, HEAD_DIM), pos).reshape(b, s, A_HEADS, 2, HEAD_DIM)
    a_v = a_v.reshape(b, s, A_HEADS, 2 * HEAD_DIM)
    f32 = jnp.float32
    lam = (jnp.exp(jnp.sum(lq1.astype(f32) * lk1.astype(f32)))
           - jnp.exp(jnp.sum(lq2.astype(f32) * lk2.astype(f32))) + lam_init)
    a_o = rms_norm(diff_attention(a_q, a_k, a_v, lam), subln) * (1.0 - lam_init)
    b_q = axial_rope(rms_norm(b_q.reshape(b, s, B_Q_HEADS, HEAD_DIM), q_norm), row, col)
    b_k = axial_rope(rms_norm(b_k.reshape(b, s, B_KV_HEADS, HEAD_DIM), k_norm), row, col)
    b_q = b_q.reshape(b, s, B_KV_HEADS, B_Q_HEADS // B_KV_HEADS, HEAD_DIM)
    b_v = b_v.reshape(b, s, B_KV_HEADS, HEAD_DIM)
    b_o = gqa_attention(b_q, b_k, b_v)
    mix = jnp.concatenate([a_o.reshape(b, s, A_V), b_o.reshape(b, s, B_Q)], axis=-1)
    return mix @ w_out


def odd_mixer(h, w_in, rel_bias, w_out):
    b, s, _ = h.shape
    q, k, v = jnp.split((h @ w_in).reshape(b, s, 3, C_HEADS, HEAD_DIM), 3, axis=2)
    o = neighbourhood_attention(q[:, :, 0], k[:, :, 0], v[:, :, 0], rel_bias)
    return o @ w_out


def conv_ffn(h, w_up, conv_w, conv_b, w_down):
    u = h @ w_up
    up = jnp.pad(u, ((0, 0), (1, 1), (0, 0)))
    u = up[:, :-2] * conv_w[0] + up[:, 1:-1] * conv_w[1] + up[:, 2:] * conv_w[2] + conv_b
    gate, val = jnp.split(u, 2, axis=-1)
    return (jax.nn.silu(gate) * val) @ w_down


def setup_inputs(seed: int = 0) -> dict:
    key = jax.random.key(seed)
    ks = jax.random.split(key, 26)
    nrm = lambda k, shape, scale: jax.random.normal(k, shape, jnp.float32) * scale
    gain = lambda k, shape: 1.0 + 0.02 * jax.random.normal(k, shape, jnp.float32)
    return {
        "x": nrm(ks[0], (BATCH, SEQ, D_MODEL), 1.0),
        "p": nrm(ks[1], (DEPTH, BATCH, SEQ, PLE_DIM), 1.0),
        "attn_norm": gain(ks[2], (DEPTH, D_MODEL)),
        "w_in_ab": nrm(ks[3], (N_EVEN, D_MODEL, EVEN_IN), D_MODEL ** -0.5),
        "lambda_q1": nrm(ks[4], (N_EVEN, HEAD_DIM), 0.1),
        "lambda_k1": nrm(ks[5], (N_EVEN, HEAD_DIM), 0.1),
        "lambda_q2": nrm(ks[6], (N_EVEN, HEAD_DIM), 0.1),
        "lambda_k2": nrm(ks[7], (N_EVEN, HEAD_DIM), 0.1),
        "a_subln": gain(ks[8], (N_EVEN, 2 * HEAD_DIM)),
        "b_q_norm": gain(ks[9], (N_EVEN, HEAD_DIM)),
        "b_k_norm": gain(ks[10], (N_EVEN, HEAD_DIM)),
        "w_out_ab": nrm(ks[11], (N_EVEN, EVEN_MIX, D_MODEL), EVEN_MIX ** -0.5),
        "w_in_c": nrm(ks[12], (N_ODD, D_MODEL, C_IN), D_MODEL ** -0.5),
        "c_rel_bias": nrm(ks[13], (N_ODD, C_HEADS, 2 * NA_KH - 1, 2 * NA_KW - 1), 0.02),
        "w_out_c": nrm(ks[14], (N_ODD, C_MIX, D_MODEL), C_MIX ** -0.5),
        "ffn_norm": gain(ks[15], (DEPTH, D_MODEL)),
        "w_ffn_up": nrm(ks[16], (DEPTH, D_MODEL, 2 * D_FF), D_MODEL ** -0.5),
        "ffn_conv_w": nrm(ks[17], (DEPTH, CONV_W, 2 * D_FF), CONV_W ** -0.5),
        "ffn_conv_b": nrm(ks[18], (DEPTH, 2 * D_FF), 0.02),
        "w_ffn_down": nrm(ks[19], (DEPTH, D_FF, D_MODEL), D_FF ** -0.5),
        "ple_norm": gain(ks[20], (DEPTH, D_MODEL)),
        "w_ple_gate": nrm(ks[21], (DEPTH, D_MODEL, D_MODEL), D_MODEL ** -0.5),
        "w_ple_proj": nrm(ks[22], (DEPTH, PLE_DIM, D_MODEL), PLE_DIM ** -0.5),
        "final_norm": gain(ks[23], (D_MODEL,)),
    }


def reference(x, p, attn_norm, w_in_ab, lambda_q1, lambda_k1, lambda_q2, lambda_k2, a_subln,
              b_q_norm, b_k_norm, w_out_ab, w_in_c, c_rel_bias, w_out_c, ffn_norm, w_ffn_up,
              ffn_conv_w, ffn_conv_b, w_ffn_down, ple_norm, w_ple_gate, w_ple_proj, final_norm):
    s = x.shape[1]
    pos = jnp.arange(s, dtype=jnp.int32)
    row = pos // GRID_W
    col = pos % GRID_W
    for i in range(DEPTH):
        j = i // 2
        h = rms_norm(x, attn_norm[i])
        if i % 2 == 0:
            lam_init = 0.8 - 0.6 * math.exp(-0.3 * i)
            y = even_mixer(h, w_in_ab[j], lambda_q1[j], lambda_k1[j], lambda_q2[j], lambda_k2[j],
                           a_subln[j], b_q_norm[j], b_k_norm[j], w_out_ab[j], lam_init, pos, row, col)
        else:
            y = odd_mixer(h, w_in_c[j], c_rel_bias[j], w_out_c[j])
        x = x + y
        x = x + conv_ffn(rms_norm(x, ffn_norm[i]), w_ffn_up[i], ffn_conv_w[i], ffn_conv_b[i], w_ffn_down[i])
        gate = jax.nn.sigmoid(rms_norm(x, ple_norm[i]) @ w_ple_gate[i])
        x = x + gate * (p[i] @ w_ple_proj[i])
    return rms_norm(x, final_norm)
```

```python
import math
import numpy as np
import concourse.bass as bass
import concourse.mybir as mybir
from concourse.bass_utils import run_bass_kernel_spmd

F32 = mybir.dt.float32
BF16 = mybir.dt.bfloat16
U8 = mybir.dt.uint8
AF = mybir.ActivationFunctionType
ALU = mybir.AluOpType
AX = mybir.AxisListType

D = 1024
SEQ = 8192
DEPTH = 4
DFF = 2816
NFC = DFF // 128
PLE = 256
EPS = 1e-6
NEG = -30000.0
TC = 512
DBG = {}
NKT = SEQ // 128


class Buf:
    __slots__ = ("w", "r", "excl")

    def __init__(self):
        self.w = None
        self.r = []
        self.excl = False


class Sched:
    ENG = ("pe", "dve", "act", "pool", "sp")
    NPOOL = 8

    def __init__(self, nc):
        self.nc = nc
        self.q = {e: [] for e in self.ENG}
        self.sem = {e: nc.alloc_semaphore("s_" + e) for e in self.ENG}
        self.cnt = {e: 0 for e in self.ENG}
        self.known = {e: {} for e in self.ENG}
        self.dpool = {qn: [[nc.alloc_semaphore("d_%s%d" % (qn, i)), 0] for i in range(self.NPOOL)]
                      for qn in ("sp", "pool")}
        self.dpi = {"sp": 0, "pool": 0}
        self.ninst = 0

    def _wait(self, eng, tok):
        sem, val = tok
        k = self.known[eng]
        key = sem.num
        if k.get(key, 0) >= val:
            return
        k[key] = val
        self.q[eng].append(lambda e, sem=sem, val=val: e.wait_ge(sem, val))
        self.ninst += 1

    def _deps(self, eng, reads, writes):
        own = self.sem[eng].num if eng == "pe" else None
        for b in reads:
            if b.w is not None and b.w[0].num != own:
                self._wait(eng, b.w)
        for b in writes:
            if b.w is not None and b.w[0].num != own:
                self._wait(eng, b.w)
            for t in b.r:
                if t[0].num != own:
                    self._wait(eng, t)

    def _update(self, tok, reads, writes):
        for b in writes:
            b.w = tok
            b.r = []
        for b in reads:
            b.r.append(tok)
            if len(b.r) > 24:
                b.r = b.r[-24:]

    def op(self, eng, fn, reads=(), writes=(), inc=True):
        if eng == "pool" and DBG.get("nopool"):
            eng = "dve"
        if any(b.excl for b in reads):
            writes = list(writes) + [b for b in reads if b.excl]
            reads = [b for b in reads if not b.excl]
        self._deps(eng, reads, writes)
        sem = self.sem[eng]
        if inc:
            self.cnt[eng] += 1
            tok = (sem, self.cnt[eng])
            self.q[eng].append(lambda e, fn=fn, sem=sem: fn(e).then_inc(sem, 1))
        else:
            tok = (sem, self.cnt[eng] + 1)
            self.q[eng].append(lambda e, fn=fn: fn(e))
        self.ninst += 1
        self._update(tok, reads, writes)
        return tok

    def dma(self, qn, out, in_, reads=(), writes=(), slow=False):
        self._deps(qn, reads, writes)
        pool = self.dpool[qn]
        i = self.dpi[qn]
        self.dpi[qn] = (i + 1) % len(pool)
        ent = pool[i]
        if ent[1] > 0:
            self._wait(qn, (ent[0], 16 * ent[1]))
        ent[1] += 1
        sem = ent[0]
        tok = (sem, 16 * ent[1])
        if slow:
            self.q[qn].append(lambda e, out=out, in_=in_, sem=sem: e.dma_start(out=out, in_=in_, allow_slow_non_contiguous=True).then_inc(sem, 16))
        else:
            self.q[qn].append(lambda e, out=out, in_=in_, sem=sem: e.dma_start(out=out, in_=in_).then_inc(sem, 16))
        self.ninst += 1
        self._update(tok, reads, writes)
        return tok

    def all_tokens(self):
        toks = []
        for e in self.ENG:
            if self.cnt[e] > 0:
                toks.append((self.sem[e], self.cnt[e]))
        for qn in self.dpool:
            for ent in self.dpool[qn]:
                if ent[1] > 0:
                    toks.append((ent[0], 16 * ent[1]))
        return toks

    def barrier(self, engines=None):
        toks = self.all_tokens()
        for e in (engines or self.ENG):
            for t in toks:
                self._wait(e, t)

    def emit(self):
        q = self.q
        with self.nc.Block() as block:
            block.tensor(lambda e: [f(e) for f in q["pe"]])
            block.vector(lambda e: [f(e) for f in q["dve"]])
            block.scalar(lambda e: [f(e) for f in q["act"]])
            block.gpsimd(lambda e: [f(e) for f in q["pool"]])
            block.sync(lambda e: [f(e) for f in q["sp"]])


class Ring:
    def __init__(self, items):
        self.items = items
        self.i = 0

    def next(self):
        it = self.items[self.i]
        self.i = (self.i + 1) % len(self.items)
        return it


class Tile:
    def __init__(self, ap, nb=1):
        self.a = ap
        self.b = [Buf() for _ in range(nb)]

    @property
    def B(self):
        return self.b[0]


class Arena:
    def __init__(self, nc, nbytes):
        self.t = nc.alloc_sbuf_tensor("arena", [128, nbytes], U8)
        self.n = nbytes
        self.off = 0

    def reset(self):
        self.off = 0

    def tile(self, shape, dtype, nb=1):
        es = 2 if dtype == BF16 else 4
        tot = 1
        for s in shape:
            tot *= s
        n = (tot * es + 63) // 64 * 64
        assert self.off + n <= self.n, ("arena overflow", self.off, n, self.n)
        ap = self.t[:, self.off:self.off + tot * es].bitcast(dtype)
        self.off += n
        if len(shape) == 2:
            ap = ap.rearrange("p (a b) -> p a b", b=shape[1])
        elif len(shape) == 3:
            ap = ap.rearrange("p (a b c) -> p a b c", b=shape[1], c=shape[2])
        return Tile(ap, nb)


class SmallLayout:
    def __init__(self):
        self.off = {}
        self.n = 0

    def add(self, name, width):
        self.off[name] = (self.n, width)
        self.n += width


def small_layout():
    L = SmallLayout()
    for i in range(DEPTH):
        L.add("attn_norm%d" % i, 8)
        L.add("ffn_norm%d" % i, 8)
        L.add("ple_norm%d" % i, 8)
        L.add("conv_w%d" % i, 3 * 44)
        L.add("conv_b%d" % i, 44)
    L.add("final_norm", 8)
    for j in range(2):
        L.add("subln%d" % j, 1)
        L.add("qnorm%d" % j, 1)
        L.add("knorm%d" % j, 1)
        for nm in ("lq1", "lk1", "lq2", "lk2"):
            L.add("%s_%d" % (nm, j), 64)
    return L


def fm(v):
    return np.ascontiguousarray(v.reshape(-1, 128).T)


def pack_small(inp):
    L = small_layout()
    a = np.zeros((128, L.n), np.float32)

    def put(name, arr):
        o, w = L.off[name]
        assert arr.shape == (128, w), (name, arr.shape, w)
        a[:, o:o + w] = arr

    for i in range(DEPTH):
        put("attn_norm%d" % i, fm(inp["attn_norm"][i]))
        put("ffn_norm%d" % i, fm(inp["ffn_norm"][i]))
        put("ple_norm%d" % i, fm(inp["ple_norm"][i]))
        cw = inp["ffn_conv_w"][i]
        put("conv_w%d" % i, np.concatenate([fm(cw[j]) for j in range(3)], axis=1))
        put("conv_b%d" % i, fm(inp["ffn_conv_b"][i]))
    put("final_norm", fm(inp["final_norm"]))
    for j in range(2):
        put("subln%d" % j, inp["a_subln"][j].reshape(128, 1))
        put("qnorm%d" % j, np.tile(inp["b_q_norm"][j], 2).reshape(128, 1))
        put("knorm%d" % j, np.tile(inp["b_k_norm"][j], 2).reshape(128, 1))
        for nm, key in (("lq1", "lambda_q1"), ("lk1", "lambda_k1"), ("lq2", "lambda_q2"), ("lk2", "lambda_k2")):
            put("%s_%d" % (nm, j), np.broadcast_to(inp[key][j][None, :], (128, 64)))
    return a


def const_mats():
    m = np.zeros((128, 5, 128), np.float32)
    m[:, 0, :] = 1.0
    m[0:64, 1, 0:64] = 1.0
    m[64:128, 1, 64:128] = 1.0
    m[:, 2, :] = np.eye(128, dtype=np.float32)
    for mo in range(128):
        f = mo % 64
        base = mo - f
        if f < 8:
            m[base + f + 8, 3, mo] = 1.0
        elif f < 16:
            m[base + f - 8, 3, mo] = 1.0
        g = f % 32
        gb = f - g
        if g < 16:
            m[base + gb + g + 16, 4, mo] = 1.0
        else:
            m[base + gb + g - 16, 4, mo] = 1.0
    return m


def rope_tables():
    pos = np.arange(SEQ, dtype=np.float32)
    row = (np.arange(SEQ) // 64).astype(np.float32)
    col = (np.arange(SEQ) % 64).astype(np.float32)
    invA = (np.float32(500000.0) ** (-np.arange(8, dtype=np.float32) / np.float32(8))).astype(np.float32)
    invB = (np.float32(10000.0) ** (-np.arange(16, dtype=np.float32) / np.float32(16))).astype(np.float32)
    ra = np.zeros((32, SEQ), np.float32)
    for f in range(16):
        ang = (pos * invA[f % 8]).astype(np.float32)
        ra[f] = np.cos(ang)
        ra[16 + f] = -np.sin(ang) if f < 8 else np.sin(ang)
    rb = np.zeros((128, SEQ), np.float32)
    for f in range(64):
        g = f % 32
        src = row if f < 32 else col
        ang = (src * invB[g % 16]).astype(np.float32)
        rb[f] = np.cos(ang)
        rb[64 + f] = -np.sin(ang) if g < 16 else np.sin(ang)
    return ra, rb


def na_tables(bias):
    kc = np.arange(64)[:, None]
    qc = np.arange(64)[None, :]
    cs = np.clip(qc - 8, 0, 48)
    colv = (kc >= cs) & (kc < cs + 16)
    ci = np.clip(kc - qc + 15, 0, 30)
    out = np.zeros((16, 23, 64, 64), np.float32)
    for i in range(23):
        dr = 11 - i
        for h in range(16):
            if -7 <= dr <= 7:
                v = bias[h, dr + 7][ci]
            else:
                v = np.zeros((64, 64), np.float32)
            out[h, i] = np.where(colv, v, np.float32(NEG))
    return out


def na_rowmask():
    U = np.zeros((16, SEQ), np.float32)
    krow = np.arange(SEQ) // 64
    U[krow % 16, np.arange(SEQ)] = 1.0
    Vm = np.full((16, SEQ), NEG, np.float32)
    for r in range(128):
        c = r // 8
        rs = min(max(r - 4, 0), 120)
        for R in range(8 * c - 4, 8 * c + 12):
            if 0 <= R < 128 and rs <= R < rs + 8:
                Vm[R % 16, r * 64:(r + 1) * 64] = 0.0
    return U, Vm


def build(layers=(0, 1, 2, 3), final=True, ntok=SEQ, dump=False, stop=None):
    NTOK = ntok
    NCH = NTOK // TC
    nc = bass.Bass("TRN2", target_bir_lowering=False)
    L = small_layout()

    def din(name, shape, dt=F32):
        return nc.dram_tensor(name, list(shape), dt, kind="ExternalInput").ap()

    def dscr(name, shape, dt):
        return nc.dram_tensor(name, list(shape), dt, kind="Internal").ap()

    IN_SPECS = {
        "xT": [D, NTOK],
        "pT": [DEPTH, PLE, NTOK],
        "small": [128, L.n],
        "cmat": [128, 5, 128],
        "ropeA": [32, SEQ],
        "ropeB": [128, SEQ],
        "natab": [2, 16, 23, 64, 64],
        "naU": [16, SEQ],
        "naV": [16, SEQ],
        "w_in_ab": [2, D, 2304],
        "w_out_ab": [2, D, D],
        "w_in_c": [2, D, 3072],
        "w_out_c": [2, D, D],
        "w_ffn_up": [DEPTH, D, 2 * DFF],
        "w_ffn_down": [DEPTH, DFF, D],
        "w_ple_gate": [DEPTH, D, D],
        "w_ple_proj": [DEPTH, PLE, D],
    }
    _ins = {}

    def I(name):
        if name not in _ins:
            _ins[name] = din(name, IN_SPECS[name])
        return _ins[name]

    outT = nc.dram_tensor("outT", [D, NTOK], F32, kind="ExternalOutput").ap()
    dbgT = nc.dram_tensor("dbgT", [D, NTOK], F32, kind="ExternalOutput").ap() if dump else None

    xres = dscr("xres", [D, NTOK], F32)
    qT = dscr("qT", [D, NTOK], BF16)
    kT = dscr("kT", [D, SEQ], BF16)
    vtm = dscr("vtm", [SEQ, D], BF16)
    mixT = dscr("mixT", [D, NTOK], BF16)
    h2T = dscr("h2T", [D, NTOK], BF16)

    S = Sched(nc)

    small = Tile(nc.alloc_sbuf_tensor("small_sb", [128, L.n], F32)[:])
    cm = Tile(nc.alloc_sbuf_tensor("cmat_sb", [128, 5, 128], BF16)[:])
    der = Tile(nc.alloc_sbuf_tensor("derived", [128, 16], F32)[:])
    zero = Tile(nc.alloc_sbuf_tensor("zero", [128, 8, 1], BF16)[:])
    ps = [Tile(nc.alloc_psum_tensor("ps%d" % i, [128, 512], F32)[:]) for i in range(8)]
    for t_ in ps:
        t_.B.excl = True
    lamtmp = [Tile(nc.alloc_sbuf_tensor("lamtmp%d" % j, [128, 64], F32)[:]) for j in range(2)]
    arena = Arena(nc, nc.sbuf_bytes_remaining - 256)

    ones = cm.a[:, 0, :]
    blk64 = cm.a[:, 1, :]
    ident = cm.a[:, 2, :]
    RAT = cm.a[:, 3, :]
    RBT = cm.a[:, 4, :]

    def sm(name, j=None):
        o, w = L.off[name]
        if j is None:
            return small.a[:, o:o + w]
        return small.a[:, o + j:o + j + 1]

    def mm(out, lhsT, rhs, start, stop, reads, writes, inc):
        return S.op("pe", lambda e: e.matmul(out, lhsT, rhs, start=start, stop=stop), reads, writes, inc)

    def act(out, in_, func, reads, writes, bias=0.0, scale=1.0):
        return S.op("act", lambda e: e.activation(out=out, in_=in_, func=func, bias=bias, scale=scale), reads, writes)

    def tt(eng, out, in0, in1, op, reads, writes):
        return S.op(eng, lambda e: e.tensor_tensor(out=out, in0=in0, in1=in1, op=op), reads, writes)

    def stt(eng, out, in0, scalar, in1, op0, op1, reads, writes):
        return S.op(eng, lambda e: e.scalar_tensor_tensor(out=out, in0=in0, scalar=scalar, in1=in1, op0=op0, op1=op1),
                    reads, writes)

    def tcopy(eng, out, in_, reads, writes):
        return S.op(eng, lambda e: e.tensor_copy(out=out, in_=in_), reads, writes)

    def recip(out, in_, reads, writes):
        return S.op("dve", lambda e: e.reciprocal(out=out, in_=in_), reads, writes)

    S.dma("sp", small.a, I("small"), writes=[small.B])
    S.dma("pool", cm.a, I("cmat"), writes=[cm.B])
    S.op("dve", lambda e: e.memset(zero.a, 0.0), writes=[zero.B])
    for j in range(2):
        lam_init = 0.8 - 0.6 * math.exp(-0.3 * (2 * j))
        tmp = lamtmp[j]
        for n, (a, b) in enumerate((("lq1", "lk1"), ("lq2", "lk2"))):
            tt("dve", tmp.a, sm("%s_%d" % (a, j)), sm("%s_%d" % (b, j)), ALU.mult, [small.B], [tmp.B])
            S.op("dve", lambda e, o=der.a[:, 8 + 4 * j + n:9 + 4 * j + n], t=tmp.a: e.reduce_sum(out=o, in_=t, axis=AX.X),
                 [tmp.B], [der.B])
        act(der.a[:, 8 + 4 * j:10 + 4 * j], der.a[:, 8 + 4 * j:10 + 4 * j], AF.Exp, [der.B], [der.B])
        stt("dve", der.a[:, 4 * j:4 * j + 1], der.a[:, 9 + 4 * j:10 + 4 * j], -lam_init, der.a[:, 8 + 4 * j:9 + 4 * j],
            ALU.add, ALU.subtract, [der.B], [der.B])
        S.op("dve", lambda e, o=der.a[:, 4 * j + 1:4 * j + 2], i_=sm("subln%d" % j), c=1.0 - lam_init:
             e.tensor_scalar(out=o, in0=i_, scalar1=c, scalar2=None, op0=ALU.mult), [small.B, der.B], [der.B])

    def rmsnorm(xs, gname, h, scr, nfeat_inv=1.0 / D):
        sq, rt = scr["sq"], scr["rt"]
        pss = scr["psring"].next()
        for kc in range(8):
            act(sq.a[:, kc, :], xs.a[:, kc, :], AF.Square, [xs.b[kc]], [sq.b[kc]])
        for kc in range(8):
            mm(pss.a, ones, sq.a[:, kc, :], kc == 0, kc == 7, [cm.B, sq.b[kc]], [pss.B], kc == 7)
        act(rt.a, pss.a, AF.Sqrt, [pss.B], [rt.B], bias=EPS, scale=nfeat_inv)
        recip(rt.a, rt.a, [rt.B], [rt.B])
        for kc in range(8):
            stt("dve", h.a[:, kc, :], xs.a[:, kc, :], sm(gname, kc), rt.a, ALU.mult, ALU.mult,
                [xs.b[kc], rt.B, small.B], [h.b[kc]])

    def load_w(dst, src_rows, nk, qn="pool"):
        for kc in range(nk):
            S.dma(qn, dst.a[:, kc, :], src_rows[kc * 128:(kc + 1) * 128, :], writes=[dst.b[kc]])

    def phase_EA(i, first):
        S.barrier()
        arena.reset()
        do_ple = not first
        do_proj = i < DEPTH and i in layers
        do_final = (i == DEPTH)
        even = (i % 2 == 0)
        j = i // 2
        xsr = Ring([arena.tile([8, TC], F32, 8) for _ in range(2)])
        scr = {"sq": arena.tile([8, TC], BF16, 8), "rt": arena.tile([TC], F32), "psring": Ring(ps[6:8])}
        hr = Ring([arena.tile([8, TC], BF16, 8) for _ in range(2)])
        psr = Ring(ps[0:6])
        if do_ple:
            Wg = arena.tile([8, D], BF16, 8)
            Wp = arena.tile([2, D], BF16, 2)
            load_w(Wg, I("w_ple_gate")[i - 1], 8)
            load_w(Wp, I("w_ple_proj")[i - 1], 2)
            pr = Ring([arena.tile([2, TC], BF16) for _ in range(2)])
            gate_r = Ring([arena.tile([TC], F32) for _ in range(2)])
        if do_final:
            hf = arena.tile([8, TC], F32, 8)
        if do_proj:
            NW = 2304 if even else 3072
            Win = arena.tile([8, NW], BF16, 8)
            load_w(Win, (I("w_in_ab") if even else I("w_in_c"))[j], 8)
            if even:
                tabr = Ring([arena.tile([4, TC], F32, 4) for _ in range(2)])
                for t_ in tabr.items:
                    S.op("pool", lambda e, o=t_.a[:, 0, :]: e.memset(o, 1.0), writes=t_.b)
                    S.op("pool", lambda e, o=t_.a[:, 1, :]: e.memset(o, 0.0), writes=t_.b)
                qf_r = Ring([arena.tile([TC], F32) for _ in range(2)])
                qb_r = Ring([arena.tile([TC], BF16) for _ in range(2)])
                t1_r = Ring([arena.tile([TC], F32) for _ in range(2)])
                t2_r = Ring([arena.tile([TC], F32) for _ in range(2)])
                sqb_r = Ring([arena.tile([TC], BF16) for _ in range(2)])
                rs_r = Ring([arena.tile([TC], F32) for _ in range(2)])
            ob_r = Ring([arena.tile([TC], BF16) for _ in range(3)])
            vo_r = Ring([arena.tile([TC], BF16) for _ in range(2)])
        src = I("xT") if first else xres
        steps = DBG.get("ea_steps", 99)

        for c in range(min(NCH, DBG.get("ea_chunks", NCH))):
            cs = slice(c * TC, (c + 1) * TC)
            xs = xsr.next()
            S.dma("sp", xs.a, src[:, cs].rearrange("(k p) t -> p k t", p=128), writes=xs.b)
            if do_proj and even:
                tab = tabr.next()
                for n_, pb in enumerate((0, 64)):
                    S.dma("sp", tab.a[pb:pb + 16, 0:2, :], I("ropeA")[:, cs].rearrange("(f p) t -> p f t", p=16), writes=[tab.b[n_]])
                    S.dma("sp", tab.a[pb:pb + 64, 2:4, :], I("ropeB")[:, cs].rearrange("(f p) t -> p f t", p=64), writes=[tab.b[2 + n_]])
            if do_ple:
                pch = pr.next()
                S.dma("pool", pch.a, I("pT")[i - 1][:, cs].rearrange("(k p) t -> p k t", p=128), writes=[pch.B])
                h3 = hr.next()
                rmsnorm(xs, "ple_norm%d" % (i - 1), h3, scr)
                for dc in range(8):
                    ds = slice(dc * 128, (dc + 1) * 128)
                    pg = psr.next()
                    for kc in range(8):
                        mm(pg.a, Wg.a[:, kc, ds], h3.a[:, kc, :], kc == 0, kc == 7, [Wg.b[kc], h3.b[kc]], [pg.B], kc == 7)
                    gate = gate_r.next()
                    act(gate.a, pg.a, AF.Sigmoid, [pg.B], [gate.B])
                    pp = psr.next()
                    for k2 in range(2):
                        mm(pp.a, Wp.a[:, k2, ds], pch.a[:, k2, :], k2 == 0, k2 == 1, [Wp.b[k2], pch.B], [pp.B], k2 == 1)
                    tt("dve", gate.a, pp.a, gate.a, ALU.mult, [pp.B, gate.B], [gate.B])
                    tt("dve", xs.a[:, dc, :], xs.a[:, dc, :], gate.a, ALU.add, [xs.b[dc], gate.B], [xs.b[dc]])
            if do_final:
                rmsnorm(xs, "final_norm", hf, scr)
                S.dma("sp", outT[:, cs].rearrange("(k p) t -> p k t", p=128), hf.a, reads=hf.b)
                continue
            if do_ple or first:
                S.dma("sp", xres[:, cs].rearrange("(k p) t -> p k t", p=128), xs.a, reads=xs.b)
            if not do_proj or steps <= 1:
                continue
            h = hr.next()
            rmsnorm(xs, "attn_norm%d" % i, h, scr)
            if steps <= 2:
                continue

            def proj_fm(n0):
                p_ = psr.next()
                for kc in range(8):
                    mm(p_.a, Win.a[:, kc, n0:n0 + 128], h.a[:, kc, :], kc == 0, kc == 7, [Win.b[kc], h.b[kc]], [p_.B], kc == 7)
                return p_

            def rope_finish(qsrc_f32, qsrc_b, ci, RT, dst_rows, dst):
                prot = psr.next()
                mm(prot.a, RT, qsrc_b.a, True, True, [cm.B, qsrc_b.B], [prot.B], True)
                t1 = t1_r.next()
                tt("pool", t1.a, qsrc_f32.a, tab.a[:, ci, :], ALU.mult, [qsrc_f32.B] + tab.b, [t1.B])
                t2 = t2_r.next()
                tt("dve", t2.a, prot.a, tab.a[:, ci + 1, :], ALU.mult, [prot.B] + tab.b, [t2.B])
                ob = ob_r.next()
                tt("pool", ob.a, t1.a, t2.a, ALU.add, [t1.B, t2.B], [ob.B])
                if DBG.get("ea_sub", 99) <= 3:
                    return
                S.dma("sp", dst[dst_rows, cs], ob.a, reads=[ob.B])

            if even:
                sub = DBG.get("ea_sub", 99)
                for which, n_base, dst in ((0, 0, qT), (1, 512, kT)):
                    for hA in range(4):
                        p_ = proj_fm(n_base + hA * 128)
                        if sub <= 1:
                            continue
                        qf = qf_r.next()
                        act(qf.a, p_.a, AF.Identity, [p_.B], [qf.B])
                        qb = qb_r.next()
                        tcopy("dve", qb.a, p_.a, [p_.B], [qb.B])
                        if sub <= 2:
                            continue
                        rope_finish(qf, qb, 0, RAT, slice(hA * 128, (hA + 1) * 128), dst)
                if steps <= 3:
                    continue
                for n0, gname, rows, dst in ([(1536 + m * 128, "qnorm%d" % j, slice(512 + m * 128, 640 + m * 128), qT) for m in range(4)]
                                             + [(2048, "knorm%d" % j, slice(512, 640), kT)]):
                    p_ = proj_fm(n0)
                    qf = qf_r.next()
                    act(qf.a, p_.a, AF.Identity, [p_.B], [qf.B])
                    sqb = sqb_r.next()
                    act(sqb.a, p_.a, AF.Square, [p_.B], [sqb.B])
                    pss = psr.next()
                    mm(pss.a, blk64, sqb.a, True, True, [cm.B, sqb.B], [pss.B], True)
                    rs = rs_r.next()
                    act(rs.a, pss.a, AF.Sqrt, [pss.B], [rs.B], bias=EPS, scale=1.0 / 64)
                    recip(rs.a, rs.a, [rs.B], [rs.B])
                    stt("dve", qf.a, qf.a, sm(gname), rs.a, ALU.mult, ALU.mult, [qf.B, rs.B, small.B], [qf.B])
                    qb = qb_r.next()
                    tcopy("pool", qb.a, qf.a, [qf.B], [qb.B])
                    rope_finish(qf, qb, 2, RBT, rows, dst)
                if steps <= 4:
                    continue
                for tsub in range(4):
                    tsl = slice(tsub * 128, (tsub + 1) * 128)
                    t0 = c * TC + tsub * 128
                    for (w0, wn, v0) in ((1024, 512, 0), (2176, 128, 512)):
                        p_ = psr.next()
                        for kc in range(8):
                            mm(p_.a[:, 0:wn], h.a[:, kc, tsl], Win.a[:, kc, w0:w0 + wn], kc == 0, kc == 7,
                               [Win.b[kc], h.b[kc]], [p_.B], kc == 7)
                        vo = vo_r.next()
                        S.op("act", lambda e, o=vo.a[:, 0:wn], i_=p_.a[:, 0:wn]: e.copy(out=o, in_=i_), [p_.B], [vo.B])
                        S.dma("sp", vtm[t0:t0 + 128, v0:v0 + wn], vo.a[:, 0:wn], reads=[vo.B])
            else:
                for m in range(8):
                    p_ = proj_fm(m * 128)
                    ob = ob_r.next()
                    act(ob.a, p_.a, AF.Identity, [p_.B], [ob.B], scale=0.125)
                    S.dma("sp", qT[m * 128:(m + 1) * 128, cs], ob.a, reads=[ob.B])
                for m in range(8):
                    p_ = proj_fm(1024 + m * 128)
                    ob = ob_r.next()
                    tcopy("dve", ob.a, p_.a, [p_.B], [ob.B])
                    S.dma("sp", kT[m * 128:(m + 1) * 128, cs], ob.a, reads=[ob.B])
                for tsub in range(4):
                    tsl = slice(tsub * 128, (tsub + 1) * 128)
                    t0 = c * TC + tsub * 128
                    for half in range(2):
                        p_ = psr.next()
                        w0 = 2048 + half * 512
                        for kc in range(8):
                            mm(p_.a, h.a[:, kc, tsl], Win.a[:, kc, w0:w0 + 512], kc == 0, kc == 7,
                               [Win.b[kc], h.b[kc]], [p_.B], kc == 7)
                        vo = vo_r.next()
                        if half == 0:
                            S.op("act", lambda e, o=vo.a, i_=p_.a: e.copy(out=o, in_=i_), [p_.B], [vo.B])
                        else:
                            tcopy("dve", vo.a, p_.a, [p_.B], [vo.B])
                        S.dma("sp", vtm[t0:t0 + 128, half * 512:(half + 1) * 512], vo.a, reads=[vo.B])

    def phase_B_even(i):
        S.barrier()
        arena.reset()
        j = i // 2
        KTr = Ring([arena.tile([SEQ], BF16) for _ in range(2)])
        Vr = Ring([arena.tile([NKT, 128], BF16) for _ in range(2)])
        Qr = Ring([arena.tile([TC], BF16) for _ in range(2)])
        Pr = Ring([arena.tile([TC], BF16) for _ in range(4)])
        a_r = Ring([arena.tile([TC], F32) for _ in range(2)])
        r_r = Ring([arena.tile([TC], F32) for _ in range(2)])
        d_t = arena.tile([TC], F32)
        sq_t = arena.tile([TC], BF16)
        rs_t = arena.tile([TC], F32)
        ob_r = Ring([arena.tile([TC], BF16) for _ in range(2)])
        Sr = Ring(ps[0:4])
        OZr = Ring([(ps[4], ps[5]), (ps[6], ps[7])])
        neglam = der.a[:, 4 * j:4 * j + 1]
        sg = der.a[:, 4 * j + 1:4 * j + 2]

        for hA in range(4):
            KT = KTr.next()
            S.dma("sp", KT.a, kT[hA * 128:(hA + 1) * 128, :], writes=[KT.B])
            V = Vr.next()
            S.dma("sp", V.a, vtm[:, hA * 128:(hA + 1) * 128].rearrange("(kt p) d -> p kt d", p=128), writes=[V.B])
            for qc in range(NCH):
                cs = slice(qc * TC, (qc + 1) * TC)
                Q = Qr.next()
                S.dma("sp", Q.a, qT[hA * 128:(hA + 1) * 128, cs], writes=[Q.B])
                aj = []
                for jm in range(2):
                    rows = slice(64 * jm, 64 * jm + 64)
                    O, Z = OZr.next()
                    for kt in range(NKT):
                        ks = slice(kt * 128, (kt + 1) * 128)
                        s_ = Sr.next()
                        mm(s_.a, KT.a[rows, ks], Q.a[rows, :], True, True, [KT.B, Q.B], [s_.B], True)
                        P = Pr.next()
                        act(P.a, s_.a, AF.Exp, [s_.B], [P.B], scale=0.125)
                        mm(O.a, V.a[:, kt, :], P.a, kt == 0, kt == NKT - 1, [V.B, P.B], [O.B], False)
                        mm(Z.a, ones, P.a, kt == 0, kt == NKT - 1, [cm.B, P.B], [Z.B], True)
                    r = r_r.next()
                    recip(r.a, Z.a, [Z.B], [r.B])
                    a_ = a_r.next()
                    tt("dve", a_.a, O.a, r.a, ALU.mult, [O.B, r.B], [a_.B])
                    aj.append(a_)
                stt("dve", d_t.a, aj[1].a, neglam, aj[0].a, ALU.mult, ALU.add, [aj[0].B, aj[1].B, der.B], [d_t.B])
                act(sq_t.a, d_t.a, AF.Square, [d_t.B], [sq_t.B])
                pss = Sr.next()
                mm(pss.a, ones, sq_t.a, True, True, [cm.B, sq_t.B], [pss.B], True)
                act(rs_t.a, pss.a, AF.Sqrt, [pss.B], [rs_t.B], bias=EPS, scale=1.0 / 128)
                recip(rs_t.a, rs_t.a, [rs_t.B], [rs_t.B])
                ob = ob_r.next()
                stt("dve", ob.a, d_t.a, sg, rs_t.a, ALU.mult, ALU.mult, [d_t.B, rs_t.B, der.B], [ob.B])
                S.dma("sp", mixT[hA * 128:(hA + 1) * 128, cs], ob.a, reads=[ob.B])
        for g in range(2):
            KT = KTr.next()
            S.dma("sp", KT.a[0:64, :], kT[512 + 64 * g:576 + 64 * g, :], writes=[KT.B])
            V = Vr.next()
            S.dma("sp", V.a[:, :, 0:64], vtm[:, 512 + 64 * g:576 + 64 * g].rearrange("(kt p) d -> p kt d", p=128), writes=[V.B])
            for r4 in range(4):
                hq = 4 * g + r4
                for qc in range(NCH):
                    cs = slice(qc * TC, (qc + 1) * TC)
                    Q = Qr.next()
                    S.dma("sp", Q.a[0:64, :], qT[512 + 64 * hq:576 + 64 * hq, cs], writes=[Q.B])
                    O, Z = OZr.next()
                    for kt in range(NKT):
                        ks = slice(kt * 128, (kt + 1) * 128)
                        s_ = Sr.next()
                        mm(s_.a, KT.a[0:64, ks], Q.a[0:64, :], True, True, [KT.B, Q.B], [s_.B], True)
                        P = Pr.next()
                        act(P.a, s_.a, AF.Exp, [s_.B], [P.B], scale=0.125)
                        mm(O.a[0:64, :], V.a[:, kt, 0:64], P.a, kt == 0, kt == NKT - 1, [V.B, P.B], [O.B], False)
                        mm(Z.a[0:64, :], ones[:, 0:64], P.a, kt == 0, kt == NKT - 1, [cm.B, P.B], [Z.B], True)
                    r = r_r.next()
                    recip(r.a[0:64, :], Z.a[0:64, :], [Z.B], [r.B])
                    ob = ob_r.next()
                    tt("dve", ob.a[0:64, :], O.a[0:64, :], r.a[0:64, :], ALU.mult, [O.B, r.B], [ob.B])
                    S.dma("sp", mixT[512 + 64 * hq:576 + 64 * hq, cs], ob.a[0:64, :], reads=[ob.B])

    def phase_B_odd(i):
        S.barrier()
        arena.reset()
        j = i // 2
        KAr = Ring([arena.tile([SEQ], BF16) for _ in range(2)])
        QAr = Ring([arena.tile([NTOK], BF16) for _ in range(2)])
        Vr = Ring([arena.tile([NKT, 64], BF16) for _ in range(2)])
        Tr = Ring([arena.tile([8, TC], BF16, 16) for _ in range(2)])
        Pr = Ring([arena.tile([TC], BF16) for _ in range(4)])
        r_r = Ring([arena.tile([TC], F32) for _ in range(2)])
        ob_r = Ring([arena.tile([TC], BF16) for _ in range(2)])
        Sr = Ring(ps[0:4])
        OZr = Ring([(ps[4], ps[5]), (ps[6], ps[7])])
        for t_ in KAr.items:
            S.dma("pool", t_.a[64:80, :], I("naU"), writes=[t_.B])
        for t_ in QAr.items:
            S.dma("pool", t_.a[64:80, :], I("naV")[:, 0:NTOK], writes=[t_.B])
        for h in range(16):
            KA = KAr.next()
            S.dma("sp", KA.a[0:64, :], kT[64 * h:64 * h + 64, :], writes=[KA.B])
            QA = QAr.next()
            S.dma("sp", QA.a[0:64, :], qT[64 * h:64 * h + 64, :], writes=[QA.B])
            V = Vr.next()
            S.dma("sp", V.a, vtm[:, 64 * h:64 * h + 64].rearrange("(kt p) d -> p kt d", p=128), writes=[V.B])
            T = Tr.next()
            for a_ in range(2):
                for jj in range(8):
                    i0 = 15 - 2 * jj - a_
                    S.dma("pool", T.a[64 * a_:64 * a_ + 64, jj, :].rearrange("k (r q) -> k r q", q=64),
                          I("natab")[j, h, i0:i0 + 8].rearrange("r k q -> k r q"), writes=[T.b[a_ * 8 + jj]])
            for c in range(NCH):
                cs = slice(c * TC, (c + 1) * TC)
                jjs = [jj for jj in range(8) if 0 <= 4 * c - 2 + jj < NKT]
                O, Z = OZr.next()
                for n, jj in enumerate(jjs):
                    gt = 4 * c - 2 + jj
                    ks = slice(gt * 128, (gt + 1) * 128)
                    s_ = Sr.next()
                    mm(s_.a, KA.a[0:80, ks], QA.a[0:80, cs], True, False, [KA.B, QA.B], [s_.B], False)
                    mm(s_.a, ident, T.a[:, jj, :], False, True, [cm.B, T.b[jj], T.b[8 + jj]], [s_.B], True)
                    P = Pr.next()
                    act(P.a, s_.a, AF.Exp, [s_.B], [P.B])
                    mm(O.a[0:64, :], V.a[:, gt, :], P.a, n == 0, n == len(jjs) - 1, [V.B, P.B], [O.B], False)
                    mm(Z.a[0:64, :], ones[:, 0:64], P.a, n == 0, n == len(jjs) - 1, [cm.B, P.B], [Z.B], True)
                r = r_r.next()
                recip(r.a[0:64, :], Z.a[0:64, :], [Z.B], [r.B])
                ob = ob_r.next()
                tt("dve", ob.a[0:64, :], O.a[0:64, :], r.a[0:64, :], ALU.mult, [O.B, r.B], [ob.B])
                S.dma("sp", mixT[64 * h:64 * h + 64, cs], ob.a[0:64, :], reads=[ob.B])

    def phase_C(i):
        S.barrier()
        arena.reset()
        j = i // 2
        Wo = arena.tile([8, D], BF16, 8)
        load_w(Wo, (I("w_out_ab") if i % 2 == 0 else I("w_out_c"))[j], 8)
        xsr = Ring([arena.tile([8, TC], F32, 8) for _ in range(2)])
        mxr = Ring([arena.tile([8, TC], BF16) for _ in range(2)])
        hr = Ring([arena.tile([8, TC], BF16, 8) for _ in range(2)])
        scr = {"sq": arena.tile([8, TC], BF16, 8), "rt": arena.tile([TC], F32), "psring": Ring(ps[6:8])}
        psr = Ring(ps[0:6])
        for c in range(NCH):
            cs = slice(c * TC, (c + 1) * TC)
            xs = xsr.next()
            S.dma("sp", xs.a, xres[:, cs].rearrange("(k p) t -> p k t", p=128), writes=xs.b)
            mx = mxr.next()
            S.dma("sp", mx.a, mixT[:, cs].rearrange("(k p) t -> p k t", p=128), writes=[mx.B])
            for dc in range(8):
                ds = slice(dc * 128, (dc + 1) * 128)
                p_ = psr.next()
                for kc in range(8):
                    mm(p_.a, Wo.a[:, kc, ds], mx.a[:, kc, :], kc == 0, kc == 7, [Wo.b[kc], mx.B], [p_.B], kc == 7)
                tt("dve", xs.a[:, dc, :], p_.a, xs.a[:, dc, :], ALU.add, [p_.B, xs.b[dc]], [xs.b[dc]])
            S.dma("sp", xres[:, cs].rearrange("(k p) t -> p k t", p=128), xs.a, reads=xs.b)
            h2 = hr.next()
            rmsnorm(xs, "ffn_norm%d" % i, h2, scr)
            S.dma("sp", h2T[:, cs].rearrange("(k p) t -> p k t", p=128), h2.a, reads=h2.b)

    def phase_D(i):
        S.barrier()
        arena.reset()
        Wu = arena.tile([8, 2 * DFF], BF16, 8)
        Wd = arena.tile([NFC, D], BF16, NFC)
        load_w(Wu, I("w_ffn_up")[i], 8)
        load_w(Wd, I("w_ffn_down")[i], NFC)
        h2r = Ring([arena.tile([8, TC + 2], BF16) for _ in range(2)])
        g_t = arena.tile([NFC, TC], BF16, NFC)
        cg_r = Ring([arena.tile([256], F32) for _ in range(2)])
        cv_r = Ring([arena.tile([256], F32) for _ in range(2)])
        sg_r = Ring([arena.tile([256], F32) for _ in range(2)])
        xd_r = Ring([arena.tile([TC], F32) for _ in range(2)])
        psr = Ring(ps[0:8])
        cwo, _ = L.off["conv_w%d" % i]
        cbo, _ = L.off["conv_b%d" % i]

        def cw(jtap, m):
            return small.a[:, cwo + jtap * 44 + m:cwo + jtap * 44 + m + 1]

        def cb(m):
            return small.a[:, cbo + m:cbo + m + 1]

        for c in range(NCH):
            cs = slice(c * TC, (c + 1) * TC)
            h2 = h2r.next()
            lo = max(c * TC - 1, 0)
            hi = min(c * TC + TC + 1, NTOK)
            o0 = lo - (c * TC - 1)
            S.dma("sp", h2.a[:, :, o0:o0 + hi - lo], h2T[:, lo:hi].rearrange("(k p) t -> p k t", p=128), writes=[h2.B])
            if o0 > 0:
                S.op("dve", lambda e, o=h2.a[:, :, 0:1]: e.memset(o, 0.0), writes=[h2.B])
            if hi - lo + o0 < TC + 2:
                S.op("dve", lambda e, o=h2.a[:, :, TC + 1:TC + 2]: e.memset(o, 0.0), writes=[h2.B])
            for fc in range(NFC):
                for sub in range(2):
                    c0 = 256 * sub
                    res = []
                    for part, (ring_) in enumerate((cg_r, cv_r)):
                        m = part * NFC + fc
                        n0 = part * DFF + fc * 128
                        pu = psr.next()
                        for kc in range(8):
                            mm(pu.a[:, 0:258], Wu.a[:, kc, n0:n0 + 128], h2.a[:, kc, c0:c0 + 258], kc == 0, kc == 7,
                               [Wu.b[kc], h2.B], [pu.B], kc == 7)
                        cc = ring_.next()
                        act(cc.a, pu.a[:, 1:257], AF.Identity, [pu.B, small.B], [cc.B], bias=cb(m), scale=cw(1, m))
                        stt("dve", cc.a, pu.a[:, 0:256], cw(0, m), cc.a, ALU.mult, ALU.add, [pu.B, cc.B, small.B], [cc.B])
                        stt("dve", cc.a, pu.a[:, 2:258], cw(2, m), cc.a, ALU.mult, ALU.add, [pu.B, cc.B, small.B], [cc.B])
                        res.append(cc)
                    sg_ = sg_r.next()
                    act(sg_.a, res[0].a, AF.Silu, [res[0].B], [sg_.B])
                    tt("pool", g_t.a[:, fc, c0:c0 + 256], sg_.a, res[1].a, ALU.mult, [sg_.B, res[1].B], [g_t.b[fc]])
            for dc in range(8):
                ds = slice(dc * 128, (dc + 1) * 128)
                xd = xd_r.next()
                S.dma("sp", xd.a, xres[ds, cs], writes=[xd.B])
                p_ = psr.next()
                for fc in range(NFC):
                    mm(p_.a, Wd.a[:, fc, ds], g_t.a[:, fc, :], fc == 0, fc == NFC - 1, [Wd.b[fc], g_t.b[fc]], [p_.B], fc == NFC - 1)
                tt("dve", xd.a, p_.a, xd.a, ALU.add, [p_.B, xd.B], [xd.B])
                S.dma("sp", xres[ds, cs], xd.a, reads=[xd.B])

    first = True
    for i in layers:
        if stop == "P":
            break
        phase_EA(i, first)
        first = False
        if stop == "EA":
            break
        if i % 2 == 0:
            phase_B_even(i)
        else:
            phase_B_odd(i)
        if stop == "B":
            break
        phase_C(i)
        if dump and i == layers[-1]:
            S.barrier()
            for c in range(NCH):
                S.dma("sp", dbgT[:, c * TC:(c + 1) * TC], xres[:, c * TC:(c + 1) * TC])
        if stop == "C":
            break
        phase_D(i)
    if final and stop is None:
        phase_EA(layers[-1] + 1 if layers[-1] + 1 < DEPTH else DEPTH, False)
    if dump:
        S.barrier()
        for c in range(NCH):
            cs = slice(c * TC, (c + 1) * TC)
            S.dma("sp", outT[:, cs], xres[:, cs])
    S.barrier()
    S.emit()
    S.used_inputs = list(_ins.keys())
    return nc, S


_CACHE = {}


def host_consts(inp):
    U, Vm = na_rowmask()
    return {
        "small": pack_small(inp),
        "cmat": const_mats(),
        "ropeA": rope_tables()[0],
        "ropeB": rope_tables()[1],
        "natab": np.stack([na_tables(np.asarray(inp["c_rel_bias"][j])) for j in range(2)]),
        "naU": U,
        "naV": Vm,
    }


WNAMES = ("w_in_ab", "w_out_ab", "w_in_c", "w_out_c", "w_ffn_up", "w_ffn_down", "w_ple_gate", "w_ple_proj")


def kernel(**inp):
    inp = {k: np.asarray(v) for k, v in inp.items()}
    x = inp["x"]
    p = inp["p"]
    B = x.shape[0]
    if "nc" not in _CACHE:
        _CACHE["nc"] = build()[0]
    nc = _CACHE["nc"]
    consts = host_consts(inp)
    in_maps = []
    for b in range(B):
        m = dict(consts)
        m["xT"] = np.ascontiguousarray(x[b].T)
        m["pT"] = np.ascontiguousarray(np.transpose(p[:, b], (0, 2, 1)))
        for w in WNAMES:
            m[w] = np.ascontiguousarray(inp[w], dtype=np.float32)
        in_maps.append(m)
    res = run_bass_kernel_spmd(nc, in_maps, core_ids=list(range(B)))
    out = np.stack([np.ascontiguousarray(res.results[b]["outT"].T) for b in range(B)])
    return out.astype(np.float32)
```

```python
import math
import numpy as np
import concourse.bass as bass
import concourse.mybir as mybir
from concourse.bass_utils import run_bass_kernel_spmd

F32 = mybir.dt.float32
BF16 = mybir.dt.bfloat16
U8 = mybir.dt.uint8
AF = mybir.ActivationFunctionType
ALU = mybir.AluOpType
AX = mybir.AxisListType

D = 1024
SEQ = 8192
DEPTH = 4
DFF = 2816
NFC = DFF // 128
PLE = 256
EPS = 1e-6
NEG = -30000.0
TC = 512
DBG = {}
NKT = SEQ // 128


class Buf:
    __slots__ = ("w", "r", "excl")

    def __init__(self):
        self.w = None
        self.r = []
        self.excl = False


class Sched:
    ENG = ("pe", "dve", "act", "pool", "sp")
    NPOOL = 8

    def __init__(self, nc):
        self.nc = nc
        self.q = {e: [] for e in self.ENG}
        self.sem = {e: nc.alloc_semaphore("s_" + e) for e in self.ENG}
        self.cnt = {e: 0 for e in self.ENG}
        self.known = {e: {} for e in self.ENG}
        self.dpool = {qn: [[nc.alloc_semaphore("d_%s%d" % (qn, i)), 0] for i in range(self.NPOOL)]
                      for qn in ("sp", "pool")}
        self.dpi = {"sp": 0, "pool": 0}
        self.ninst = 0

    def _wait(self, eng, tok):
        sem, val = tok
        k = self.known[eng]
        key = sem.num
        if k.get(key, 0) >= val:
            return
        k[key] = val
        self.q[eng].append(lambda e, sem=sem, val=val: e.wait_ge(sem, val))
        self.ninst += 1

    def _deps(self, eng, reads, writes):
        own = self.sem[eng].num
        lim = 1 << 60 if eng == "pe" else self.cnt[eng] - 3

        def need(t):
            return t[0].num != own or t[1] > lim

        for b in reads:
            if b.w is not None and need(b.w):
                self._wait(eng, b.w)
        for b in writes:
            if b.w is not None and need(b.w):
                self._wait(eng, b.w)
            for t in b.r:
                if need(t):
                    self._wait(eng, t)

    def _update(self, tok, reads, writes):
        for b in writes:
            b.w = tok
            b.r = []
        for b in reads:
            b.r.append(tok)
            if len(b.r) > 24:
                b.r = b.r[-24:]

    def op(self, eng, fn, reads=(), writes=(), inc=True):
        if eng == "pool" and DBG.get("nopool"):
            eng = "dve"
        if any(b.excl for b in reads):
            writes = list(writes) + [b for b in reads if b.excl]
            reads = [b for b in reads if not b.excl]
        self._deps(eng, reads, writes)
        sem = self.sem[eng]
        if inc:
            self.cnt[eng] += 1
            tok = (sem, self.cnt[eng])
            self.q[eng].append(lambda e, fn=fn, sem=sem: fn(e).then_inc(sem, 1))
        else:
            tok = (sem, self.cnt[eng] + 1)
            self.q[eng].append(lambda e, fn=fn: fn(e))
        self.ninst += 1
        self._update(tok, reads, writes)
        return tok

    def dma(self, qn, out, in_, reads=(), writes=(), slow=False):
        self._deps(qn, reads, writes)
        pool = self.dpool[qn]
        i = self.dpi[qn]
        self.dpi[qn] = (i + 1) % len(pool)
        ent = pool[i]
        if ent[1] > 0:
            self._wait(qn, (ent[0], 16 * ent[1]))
        ent[1] += 1
        sem = ent[0]
        tok = (sem, 16 * ent[1])
        if slow:
            self.q[qn].append(lambda e, out=out, in_=in_, sem=sem: e.dma_start(out=out, in_=in_, allow_slow_non_contiguous=True).then_inc(sem, 16))
        else:
            self.q[qn].append(lambda e, out=out, in_=in_, sem=sem: e.dma_start(out=out, in_=in_).then_inc(sem, 16))
        self.ninst += 1
        self._update(tok, reads, writes)
        return tok

    def all_tokens(self):
        toks = []
        for e in self.ENG:
            if self.cnt[e] > 0:
                toks.append((self.sem[e], self.cnt[e]))
        for qn in self.dpool:
            for ent in self.dpool[qn]:
                if ent[1] > 0:
                    toks.append((ent[0], 16 * ent[1]))
        return toks

    def barrier(self, engines=None):
        toks = self.all_tokens()
        for e in (engines or self.ENG):
            for t in toks:
                self._wait(e, t)

    def emit(self):
        q = self.q
        with self.nc.Block() as block:
            block.tensor(lambda e: [f(e) for f in q["pe"]])
            block.vector(lambda e: [f(e) for f in q["dve"]])
            block.scalar(lambda e: [f(e) for f in q["act"]])
            block.gpsimd(lambda e: [f(e) for f in q["pool"]])
            block.sync(lambda e: [f(e) for f in q["sp"]])


class Ring:
    def __init__(self, items):
        self.items = items
        self.i = 0

    def next(self):
        it = self.items[self.i]
        self.i = (self.i + 1) % len(self.items)
        return it


class Tile:
    def __init__(self, ap, nb=1):
        self.a = ap
        self.b = [Buf() for _ in range(nb)]

    @property
    def B(self):
        return self.b[0]


class Arena:
    def __init__(self, nc, nbytes):
        self.t = nc.alloc_sbuf_tensor("arena", [128, nbytes], U8)
        self.n = nbytes
        self.off = 0

    def reset(self):
        self.off = 0

    def tile(self, shape, dtype, nb=1):
        es = 2 if dtype == BF16 else 4
        tot = 1
        for s in shape:
            tot *= s
        n = (tot * es + 63) // 64 * 64
        assert self.off + n <= self.n, ("arena overflow", self.off, n, self.n)
        ap = self.t[:, self.off:self.off + tot * es].bitcast(dtype)
        self.off += n
        if len(shape) == 2:
            ap = ap.rearrange("p (a b) -> p a b", b=shape[1])
        elif len(shape) == 3:
            ap = ap.rearrange("p (a b c) -> p a b c", b=shape[1], c=shape[2])
        return Tile(ap, nb)


class SmallLayout:
    def __init__(self):
        self.off = {}
        self.n = 0

    def add(self, name, width):
        self.off[name] = (self.n, width)
        self.n += width


def small_layout():
    L = SmallLayout()
    for i in range(DEPTH):
        L.add("attn_norm%d" % i, 8)
        L.add("ffn_norm%d" % i, 8)
        L.add("ple_norm%d" % i, 8)
        L.add("conv_w%d" % i, 3 * 44)
        L.add("conv_b%d" % i, 44)
    L.add("final_norm", 8)
    for j in range(2):
        L.add("subln%d" % j, 1)
        L.add("qnorm%d" % j, 1)
        L.add("knorm%d" % j, 1)
        for nm in ("lq1", "lk1", "lq2", "lk2"):
            L.add("%s_%d" % (nm, j), 64)
    return L


def fm(v):
    return np.ascontiguousarray(v.reshape(-1, 128).T)


def pack_small(inp):
    L = small_layout()
    a = np.zeros((128, L.n), np.float32)

    def put(name, arr):
        o, w = L.off[name]
        assert arr.shape == (128, w), (name, arr.shape, w)
        a[:, o:o + w] = arr

    for i in range(DEPTH):
        put("attn_norm%d" % i, fm(inp["attn_norm"][i]))
        put("ffn_norm%d" % i, fm(inp["ffn_norm"][i]))
        put("ple_norm%d" % i, fm(inp["ple_norm"][i]))
        cw = inp["ffn_conv_w"][i]
        put("conv_w%d" % i, np.concatenate([fm(cw[j]) for j in range(3)], axis=1))
        put("conv_b%d" % i, fm(inp["ffn_conv_b"][i]))
    put("final_norm", fm(inp["final_norm"]))
    for j in range(2):
        put("subln%d" % j, inp["a_subln"][j].reshape(128, 1))
        put("qnorm%d" % j, np.tile(inp["b_q_norm"][j], 2).reshape(128, 1))
        put("knorm%d" % j, np.tile(inp["b_k_norm"][j], 2).reshape(128, 1))
        for nm, key in (("lq1", "lambda_q1"), ("lk1", "lambda_k1"), ("lq2", "lambda_q2"), ("lk2", "lambda_k2")):
            put("%s_%d" % (nm, j), np.broadcast_to(inp[key][j][None, :], (128, 64)))
    return a


def const_mats():
    m = np.zeros((128, 5, 128), np.float32)
    m[:, 0, :] = 1.0
    m[0:64, 1, 0:64] = 1.0
    m[64:128, 1, 64:128] = 1.0
    m[:, 2, :] = np.eye(128, dtype=np.float32)
    for mo in range(128):
        f = mo % 64
        base = mo - f
        if f < 8:
            m[base + f + 8, 3, mo] = 1.0
        elif f < 16:
            m[base + f - 8, 3, mo] = 1.0
        g = f % 32
        gb = f - g
        if g < 16:
            m[base + gb + g + 16, 4, mo] = 1.0
        else:
            m[base + gb + g - 16, 4, mo] = 1.0
    return m


def rope_tables():
    pos = np.arange(SEQ, dtype=np.float32)
    row = (np.arange(SEQ) // 64).astype(np.float32)
    col = (np.arange(SEQ) % 64).astype(np.float32)
    invA = (np.float32(500000.0) ** (-np.arange(8, dtype=np.float32) / np.float32(8))).astype(np.float32)
    invB = (np.float32(10000.0) ** (-np.arange(16, dtype=np.float32) / np.float32(16))).astype(np.float32)
    ra = np.zeros((32, SEQ), np.float32)
    for f in range(16):
        ang = (pos * invA[f % 8]).astype(np.float32)
        ra[f] = np.cos(ang)
        ra[16 + f] = -np.sin(ang) if f < 8 else np.sin(ang)
    rb = np.zeros((128, SEQ), np.float32)
    for f in range(64):
        g = f % 32
        src = row if f < 32 else col
        ang = (src * invB[g % 16]).astype(np.float32)
        rb[f] = np.cos(ang)
        rb[64 + f] = -np.sin(ang) if g < 16 else np.sin(ang)
    return ra, rb


def na_tables(bias):
    kc = np.arange(64)[:, None]
    qc = np.arange(64)[None, :]
    cs = np.clip(qc - 8, 0, 48)
    colv = (kc >= cs) & (kc < cs + 16)
    ci = np.clip(kc - qc + 15, 0, 30)
    out = np.zeros((16, 23, 64, 64), np.float32)
    for i in range(23):
        dr = 11 - i
        for h in range(16):
            if -7 <= dr <= 7:
                v = bias[h, dr + 7][ci]
            else:
                v = np.zeros((64, 64), np.float32)
            out[h, i] = np.where(colv, v, np.float32(NEG))
    return out


def na_rowmask():
    U = np.zeros((16, SEQ), np.float32)
    krow = np.arange(SEQ) // 64
    U[krow % 16, np.arange(SEQ)] = 1.0
    Vm = np.full((16, SEQ), NEG, np.float32)
    for r in range(128):
        c = r // 8
        rs = min(max(r - 4, 0), 120)
        for R in range(8 * c - 4, 8 * c + 12):
            if 0 <= R < 128 and rs <= R < rs + 8:
                Vm[R % 16, r * 64:(r + 1) * 64] = 0.0
    return U, Vm


def build(layers=(0, 1, 2, 3), final=True, ntok=SEQ, dump=False, stop=None):
    NTOK = ntok
    NCH = NTOK // TC
    nc = bass.Bass("TRN2", target_bir_lowering=False)
    L = small_layout()

    def din(name, shape, dt=F32):
        return nc.dram_tensor(name, list(shape), dt, kind="ExternalInput").ap()

    def dscr(name, shape, dt):
        return nc.dram_tensor(name, list(shape), dt, kind="Internal").ap()

    IN_SPECS = {
        "xT": [D, NTOK],
        "pT": [DEPTH, PLE, NTOK],
        "small": [128, L.n],
        "cmat": [128, 5, 128],
        "ropeA": [32, SEQ],
        "ropeB": [128, SEQ],
        "natab": [2, 16, 23, 64, 64],
        "naU": [16, SEQ],
        "naV": [16, SEQ],
        "w_in_ab": [2, D, 2304],
        "w_out_ab": [2, D, D],
        "w_in_c": [2, D, 3072],
        "w_out_c": [2, D, D],
        "w_ffn_up": [DEPTH, D, 2 * DFF],
        "w_ffn_down": [DEPTH, DFF, D],
        "w_ple_gate": [DEPTH, D, D],
        "w_ple_proj": [DEPTH, PLE, D],
    }
    _ins = {}

    def I(name):
        if name not in _ins:
            _ins[name] = din(name, IN_SPECS[name])
        return _ins[name]

    outT = nc.dram_tensor("outT", [D, NTOK], F32, kind="ExternalOutput").ap()
    dbgT = nc.dram_tensor("dbgT", [D, NTOK], F32, kind="ExternalOutput").ap() if dump else None

    xres = dscr("xres", [D, NTOK], F32)
    qT = dscr("qT", [D, NTOK], BF16)
    kT = dscr("kT", [D, SEQ], BF16)
    vtm = dscr("vtm", [SEQ, D], BF16)
    mixT = dscr("mixT", [D, NTOK], BF16)
    h2T = dscr("h2T", [D, NTOK], BF16)

    S = Sched(nc)

    small = Tile(nc.alloc_sbuf_tensor("small_sb", [128, L.n], F32)[:])
    cm = Tile(nc.alloc_sbuf_tensor("cmat_sb", [128, 5, 128], BF16)[:])
    der = Tile(nc.alloc_sbuf_tensor("derived", [128, 16], F32)[:])
    zero = Tile(nc.alloc_sbuf_tensor("zero", [128, 8, 1], BF16)[:])
    pp_h = [nc.alloc_psum_tensor("pp%d" % i, [128, 1024], F32) for i in range(4)]
    ps = []
    for k_ in range(4):
        for u_ in range(2):
            t_ = Tile(pp_h[k_][:, u_ * 512:(u_ + 1) * 512])
            t_.B.excl = True
            ps.append(t_)
    PP = []
    for k_ in range(4):
        t_ = Tile(pp_h[k_][:])
        t_.b = [ps[2 * k_].B, ps[2 * k_ + 1].B]
        PP.append(t_)
    lamtmp = [Tile(nc.alloc_sbuf_tensor("lamtmp%d" % j, [128, 64], F32)[:]) for j in range(2)]
    arena = Arena(nc, nc.sbuf_bytes_remaining - 256)

    ones = cm.a[:, 0, :]
    blk64 = cm.a[:, 1, :]
    ident = cm.a[:, 2, :]
    RAT = cm.a[:, 3, :]
    RBT = cm.a[:, 4, :]

    def sm(name, j=None):
        o, w = L.off[name]
        if j is None:
            return small.a[:, o:o + w]
        return small.a[:, o + j:o + j + 1]

    def mm(out, lhsT, rhs, start, stop, reads, writes, inc):
        return S.op("pe", lambda e: e.matmul(out, lhsT, rhs, start=start, stop=stop), reads, writes, inc)

    def act(out, in_, func, reads, writes, bias=0.0, scale=1.0):
        return S.op("act", lambda e: e.activation(out=out, in_=in_, func=func, bias=bias, scale=scale), reads, writes)

    def tt(eng, out, in0, in1, op, reads, writes):
        return S.op(eng, lambda e: e.tensor_tensor(out=out, in0=in0, in1=in1, op=op), reads, writes)

    def stt(eng, out, in0, scalar, in1, op0, op1, reads, writes):
        return S.op(eng, lambda e: e.scalar_tensor_tensor(out=out, in0=in0, scalar=scalar, in1=in1, op0=op0, op1=op1),
                    reads, writes)

    def tcopy(eng, out, in_, reads, writes):
        return S.op(eng, lambda e: e.tensor_copy(out=out, in_=in_), reads, writes)

    def recip(out, in_, reads, writes):
        return S.op("dve", lambda e: e.reciprocal(out=out, in_=in_), reads, writes)

    S.dma("sp", small.a, I("small"), writes=[small.B])
    S.dma("pool", cm.a, I("cmat"), writes=[cm.B])
    S.op("dve", lambda e: e.memset(zero.a, 0.0), writes=[zero.B])
    for j in range(2):
        lam_init = 0.8 - 0.6 * math.exp(-0.3 * (2 * j))
        tmp = lamtmp[j]
        for n, (a, b) in enumerate((("lq1", "lk1"), ("lq2", "lk2"))):
            tt("dve", tmp.a, sm("%s_%d" % (a, j)), sm("%s_%d" % (b, j)), ALU.mult, [small.B], [tmp.B])
            S.op("dve", lambda e, o=der.a[:, 8 + 4 * j + n:9 + 4 * j + n], t=tmp.a: e.reduce_sum(out=o, in_=t, axis=AX.X),
                 [tmp.B], [der.B])
        act(der.a[:, 8 + 4 * j:10 + 4 * j], der.a[:, 8 + 4 * j:10 + 4 * j], AF.Exp, [der.B], [der.B])
        stt("dve", der.a[:, 4 * j:4 * j + 1], der.a[:, 9 + 4 * j:10 + 4 * j], -lam_init, der.a[:, 8 + 4 * j:9 + 4 * j],
            ALU.add, ALU.subtract, [der.B], [der.B])
        S.op("dve", lambda e, o=der.a[:, 4 * j + 1:4 * j + 2], i_=sm("subln%d" % j), c=1.0 - lam_init:
             e.tensor_scalar(out=o, in0=i_, scalar1=c, scalar2=None, op0=ALU.mult), [small.B, der.B], [der.B])

    def rmsnorm(xs, gname, h, scr, nfeat_inv=1.0 / D):
        sq, rt = scr["sq"], scr["rt"]
        pss = scr["psring"].next()
        for kc in range(8):
            act(sq.a[:, kc, :], xs.a[:, kc, :], AF.Square, [xs.b[kc]], [sq.b[kc]])
        for kc in range(8):
            mm(pss.a, ones, sq.a[:, kc, :], kc == 0, kc == 7, [cm.B, sq.b[kc]], [pss.B], kc == 7)
        act(rt.a, pss.a, AF.Sqrt, [pss.B], [rt.B], bias=EPS, scale=nfeat_inv)
        recip(rt.a, rt.a, [rt.B], [rt.B])
        for kc in range(8):
            stt("dve", h.a[:, kc, :], xs.a[:, kc, :], sm(gname, kc), rt.a, ALU.mult, ALU.mult,
                [xs.b[kc], rt.B, small.B], [h.b[kc]])

    def load_w(dst, src_rows, nk, qn="pool"):
        for kc in range(nk):
            S.dma(qn, dst.a[:, kc, :], src_rows[kc * 128:(kc + 1) * 128, :], writes=[dst.b[kc]])

    def phase_EA(i, first):
        S.barrier()
        arena.reset()
        do_ple = not first
        do_proj = i < DEPTH and i in layers
        do_final = (i == DEPTH)
        even = (i % 2 == 0)
        j = i // 2
        xsr = Ring([arena.tile([8, TC], F32, 8) for _ in range(2)])
        scr = {"sq": arena.tile([8, TC], BF16, 8), "rt": arena.tile([TC], F32), "psring": Ring(ps[6:8])}
        hr = Ring([arena.tile([8, TC], BF16, 8) for _ in range(2)])
        psr = Ring(ps[0:6])
        if do_ple:
            Wg = arena.tile([8, D], BF16, 8)
            Wp = arena.tile([2, D], BF16, 2)
            load_w(Wg, I("w_ple_gate")[i - 1], 8)
            load_w(Wp, I("w_ple_proj")[i - 1], 2)
            pr = Ring([arena.tile([2, TC], BF16) for _ in range(2)])
            gate_r = Ring([arena.tile([TC], F32) for _ in range(2)])
        if do_final:
            hf = arena.tile([8, TC], F32, 8)
        if do_proj:
            NW = 2304 if even else 3072
            Win = arena.tile([8, NW], BF16, 8)
            load_w(Win, (I("w_in_ab") if even else I("w_in_c"))[j], 8)
            if even:
                tabr = Ring([arena.tile([4, TC], F32, 4) for _ in range(2)])
                for t_ in tabr.items:
                    S.op("pool", lambda e, o=t_.a[:, 0, :]: e.memset(o, 1.0), writes=t_.b)
                    S.op("pool", lambda e, o=t_.a[:, 1, :]: e.memset(o, 0.0), writes=t_.b)
                qf_r = Ring([arena.tile([TC], F32) for _ in range(2)])
                qb_r = Ring([arena.tile([TC], BF16) for _ in range(2)])
                t1_r = Ring([arena.tile([TC], F32) for _ in range(2)])
                t2_r = Ring([arena.tile([TC], F32) for _ in range(2)])
                sqb_r = Ring([arena.tile([TC], BF16) for _ in range(2)])
                rs_r = Ring([arena.tile([TC], F32) for _ in range(2)])
            ob_r = Ring([arena.tile([TC], BF16) for _ in range(3)])
            vo_r = Ring([arena.tile([TC], BF16) for _ in range(2)])
        src = I("xT") if first else xres
        steps = DBG.get("ea_steps", 99)

        for c in range(min(NCH, DBG.get("ea_chunks", NCH))):
            cs = slice(c * TC, (c + 1) * TC)
            xs = xsr.next()
            S.dma("sp", xs.a, src[:, cs].rearrange("(k p) t -> p k t", p=128), writes=xs.b)
            if do_proj and even:
                tab = tabr.next()
                for n_, pb in enumerate((0, 64)):
                    S.dma("sp", tab.a[pb:pb + 16, 0:2, :], I("ropeA")[:, cs].rearrange("(f p) t -> p f t", p=16), writes=[tab.b[n_]])
                    S.dma("sp", tab.a[pb:pb + 64, 2:4, :], I("ropeB")[:, cs].rearrange("(f p) t -> p f t", p=64), writes=[tab.b[2 + n_]])
            if do_ple:
                pch = pr.next()
                S.dma("pool", pch.a, I("pT")[i - 1][:, cs].rearrange("(k p) t -> p k t", p=128), writes=[pch.B])
                h3 = hr.next()
                rmsnorm(xs, "ple_norm%d" % (i - 1), h3, scr)
                for dc in range(8):
                    ds = slice(dc * 128, (dc + 1) * 128)
                    pg = psr.next()
                    for kc in range(8):
                        mm(pg.a, Wg.a[:, kc, ds], h3.a[:, kc, :], kc == 0, kc == 7, [Wg.b[kc], h3.b[kc]], [pg.B], kc == 7)
                    gate = gate_r.next()
                    act(gate.a, pg.a, AF.Sigmoid, [pg.B], [gate.B])
                    pp = psr.next()
                    for k2 in range(2):
                        mm(pp.a, Wp.a[:, k2, ds], pch.a[:, k2, :], k2 == 0, k2 == 1, [Wp.b[k2], pch.B], [pp.B], k2 == 1)
                    tt("dve", gate.a, pp.a, gate.a, ALU.mult, [pp.B, gate.B], [gate.B])
                    tt("dve", xs.a[:, dc, :], xs.a[:, dc, :], gate.a, ALU.add, [xs.b[dc], gate.B], [xs.b[dc]])
            if do_final:
                rmsnorm(xs, "final_norm", hf, scr)
                S.dma("sp", outT[:, cs].rearrange("(k p) t -> p k t", p=128), hf.a, reads=hf.b)
                continue
            if do_ple or first:
                S.dma("sp", xres[:, cs].rearrange("(k p) t -> p k t", p=128), xs.a, reads=xs.b)
            if not do_proj or steps <= 1:
                continue
            h = hr.next()
            rmsnorm(xs, "attn_norm%d" % i, h, scr)
            if steps <= 2:
                continue

            def proj_fm(n0):
                p_ = psr.next()
                for kc in range(8):
                    mm(p_.a, Win.a[:, kc, n0:n0 + 128], h.a[:, kc, :], kc == 0, kc == 7, [Win.b[kc], h.b[kc]], [p_.B], kc == 7)
                return p_

            def rope_finish(qsrc_f32, qsrc_b, ci, RT, dst_rows, dst):
                prot = psr.next()
                mm(prot.a, RT, qsrc_b.a, True, True, [cm.B, qsrc_b.B], [prot.B], True)
                t1 = t1_r.next()
                tt("pool", t1.a, qsrc_f32.a, tab.a[:, ci, :], ALU.mult, [qsrc_f32.B] + tab.b, [t1.B])
                t2 = t2_r.next()
                tt("dve", t2.a, prot.a, tab.a[:, ci + 1, :], ALU.mult, [prot.B] + tab.b, [t2.B])
                ob = ob_r.next()
                tt("pool", ob.a, t1.a, t2.a, ALU.add, [t1.B, t2.B], [ob.B])
                if DBG.get("ea_sub", 99) <= 3:
                    return
                S.dma("sp", dst[dst_rows, cs], ob.a, reads=[ob.B])

            if even:
                sub = DBG.get("ea_sub", 99)
                for which, n_base, dst in ((0, 0, qT), (1, 512, kT)):
                    for hA in range(4):
                        p_ = proj_fm(n_base + hA * 128)
                        if sub <= 1:
                            continue
                        qf = qf_r.next()
                        act(qf.a, p_.a, AF.Identity, [p_.B], [qf.B])
                        qb = qb_r.next()
                        tcopy("dve", qb.a, p_.a, [p_.B], [qb.B])
                        if sub <= 2:
                            continue
                        rope_finish(qf, qb, 0, RAT, slice(hA * 128, (hA + 1) * 128), dst)
                if steps <= 3:
                    continue
                for n0, gname, rows, dst in ([(1536 + m * 128, "qnorm%d" % j, slice(512 + m * 128, 640 + m * 128), qT) for m in range(4)]
                                             + [(2048, "knorm%d" % j, slice(512, 640), kT)]):
                    p_ = proj_fm(n0)
                    qf = qf_r.next()
                    act(qf.a, p_.a, AF.Identity, [p_.B], [qf.B])
                    sqb = sqb_r.next()
                    act(sqb.a, p_.a, AF.Square, [p_.B], [sqb.B])
                    pss = psr.next()
                    mm(pss.a, blk64, sqb.a, True, True, [cm.B, sqb.B], [pss.B], True)
                    rs = rs_r.next()
                    act(rs.a, pss.a, AF.Sqrt, [pss.B], [rs.B], bias=EPS, scale=1.0 / 64)
                    recip(rs.a, rs.a, [rs.B], [rs.B])
                    stt("dve", qf.a, qf.a, sm(gname), rs.a, ALU.mult, ALU.mult, [qf.B, rs.B, small.B], [qf.B])
                    qb = qb_r.next()
                    tcopy("pool", qb.a, qf.a, [qf.B], [qb.B])
                    rope_finish(qf, qb, 2, RBT, rows, dst)
                if steps <= 4:
                    continue
                for tsub in range(4):
                    tsl = slice(tsub * 128, (tsub + 1) * 128)
                    t0 = c * TC + tsub * 128
                    for (w0, wn, v0) in ((1024, 512, 0), (2176, 128, 512)):
                        p_ = psr.next()
                        for kc in range(8):
                            mm(p_.a[:, 0:wn], h.a[:, kc, tsl], Win.a[:, kc, w0:w0 + wn], kc == 0, kc == 7,
                               [Win.b[kc], h.b[kc]], [p_.B], kc == 7)
                        vo = vo_r.next()
                        S.op("act", lambda e, o=vo.a[:, 0:wn], i_=p_.a[:, 0:wn]: e.copy(out=o, in_=i_), [p_.B], [vo.B])
                        S.dma("sp", vtm[t0:t0 + 128, v0:v0 + wn], vo.a[:, 0:wn], reads=[vo.B])
            else:
                for m in range(8):
                    p_ = proj_fm(m * 128)
                    ob = ob_r.next()
                    act(ob.a, p_.a, AF.Identity, [p_.B], [ob.B], scale=0.125)
                    S.dma("sp", qT[m * 128:(m + 1) * 128, cs], ob.a, reads=[ob.B])
                for m in range(8):
                    p_ = proj_fm(1024 + m * 128)
                    ob = ob_r.next()
                    tcopy("dve", ob.a, p_.a, [p_.B], [ob.B])
                    S.dma("sp", kT[m * 128:(m + 1) * 128, cs], ob.a, reads=[ob.B])
                for tsub in range(4):
                    tsl = slice(tsub * 128, (tsub + 1) * 128)
                    t0 = c * TC + tsub * 128
                    for half in range(2):
                        p_ = psr.next()
                        w0 = 2048 + half * 512
                        for kc in range(8):
                            mm(p_.a, h.a[:, kc, tsl], Win.a[:, kc, w0:w0 + 512], kc == 0, kc == 7,
                               [Win.b[kc], h.b[kc]], [p_.B], kc == 7)
                        vo = vo_r.next()
                        if half == 0:
                            S.op("act", lambda e, o=vo.a, i_=p_.a: e.copy(out=o, in_=i_), [p_.B], [vo.B])
                        else:
                            tcopy("dve", vo.a, p_.a, [p_.B], [vo.B])
                        S.dma("sp", vtm[t0:t0 + 128, half * 512:(half + 1) * 512], vo.a, reads=[vo.B])

    def phase_B_even(i):
        S.barrier()
        arena.reset()
        j = i // 2
        KTr = Ring([arena.tile([SEQ], BF16) for _ in range(2)])
        Vr = Ring([arena.tile([NKT, 128], BF16) for _ in range(2)])
        Qr = Ring([arena.tile([TC], BF16) for _ in range(2)])
        Pr = Ring([arena.tile([2 * TC], BF16) for _ in range(3)])
        SPr = Ring(PP[0:2])
        a_r = Ring([arena.tile([TC], F32) for _ in range(2)])
        r_r = Ring([arena.tile([TC], F32) for _ in range(2)])
        d_t = arena.tile([TC], F32)
        sq_t = arena.tile([TC], BF16)
        rs_t = arena.tile([TC], F32)
        ob_r = Ring([arena.tile([TC], BF16) for _ in range(2)])
        Sr = Ring(ps[0:4])
        OZr = Ring([(ps[4], ps[5]), (ps[6], ps[7])])
        neglam = der.a[:, 4 * j:4 * j + 1]
        sg = der.a[:, 4 * j + 1:4 * j + 2]

        for hA in range(4):
            KT = KTr.next()
            S.dma("sp", KT.a, kT[hA * 128:(hA + 1) * 128, :], writes=[KT.B])
            V = Vr.next()
            S.dma("sp", V.a, vtm[:, hA * 128:(hA + 1) * 128].rearrange("(kt p) d -> p kt d", p=128), writes=[V.B])
            for qc in range(NCH):
                cs = slice(qc * TC, (qc + 1) * TC)
                Q = Qr.next()
                S.dma("sp", Q.a, qT[hA * 128:(hA + 1) * 128, cs], writes=[Q.B])
                aj = []
                for jm in range(2):
                    rows = slice(64 * jm, 64 * jm + 64)
                    O, Z = OZr.next()
                    for kt2 in range(NKT // 2):
                        SP = SPr.next()
                        for u in range(2):
                            kt = 2 * kt2 + u
                            ks = slice(kt * 128, (kt + 1) * 128)
                            mm(SP.a[:, u * TC:(u + 1) * TC], KT.a[rows, ks], Q.a[rows, :], True, True, [KT.B, Q.B], [SP.b[u]], u == 1)
                        P = Pr.next()
                        act(P.a, SP.a, AF.Exp, SP.b, [P.B], scale=0.125)
                        for u in range(2):
                            kt = 2 * kt2 + u
                            pu_ = P.a[:, u * TC:(u + 1) * TC]
                            mm(O.a, V.a[:, kt, :], pu_, kt == 0, kt == NKT - 1, [V.B, P.B], [O.B], False)
                            mm(Z.a, ones, pu_, kt == 0, kt == NKT - 1, [cm.B, P.B], [Z.B], u == 1)
                    r = r_r.next()
                    recip(r.a, Z.a, [Z.B], [r.B])
                    a_ = a_r.next()
                    tt("dve", a_.a, O.a, r.a, ALU.mult, [O.B, r.B], [a_.B])
                    aj.append(a_)
                stt("dve", d_t.a, aj[1].a, neglam, aj[0].a, ALU.mult, ALU.add, [aj[0].B, aj[1].B, der.B], [d_t.B])
                act(sq_t.a, d_t.a, AF.Square, [d_t.B], [sq_t.B])
                pss = Sr.next()
                mm(pss.a, ones, sq_t.a, True, True, [cm.B, sq_t.B], [pss.B], True)
                act(rs_t.a, pss.a, AF.Sqrt, [pss.B], [rs_t.B], bias=EPS, scale=1.0 / 128)
                recip(rs_t.a, rs_t.a, [rs_t.B], [rs_t.B])
                ob = ob_r.next()
                stt("dve", ob.a, d_t.a, sg, rs_t.a, ALU.mult, ALU.mult, [d_t.B, rs_t.B, der.B], [ob.B])
                S.dma("sp", mixT[hA * 128:(hA + 1) * 128, cs], ob.a, reads=[ob.B])
        for g in range(2):
            KT = KTr.next()
            S.dma("sp", KT.a[0:64, :], kT[512 + 64 * g:576 + 64 * g, :], writes=[KT.B])
            V = Vr.next()
            S.dma("sp", V.a[:, :, 0:64], vtm[:, 512 + 64 * g:576 + 64 * g].rearrange("(kt p) d -> p kt d", p=128), writes=[V.B])
            for r4 in range(4):
                hq = 4 * g + r4
                for qc in range(NCH):
                    cs = slice(qc * TC, (qc + 1) * TC)
                    Q = Qr.next()
                    S.dma("sp", Q.a[0:64, :], qT[512 + 64 * hq:576 + 64 * hq, cs], writes=[Q.B])
                    O, Z = OZr.next()
                    for kt2 in range(NKT // 2):
                        SP = SPr.next()
                        for u in range(2):
                            kt = 2 * kt2 + u
                            ks = slice(kt * 128, (kt + 1) * 128)
                            mm(SP.a[:, u * TC:(u + 1) * TC], KT.a[0:64, ks], Q.a[0:64, :], True, True, [KT.B, Q.B], [SP.b[u]], u == 1)
                        P = Pr.next()
                        act(P.a, SP.a, AF.Exp, SP.b, [P.B], scale=0.125)
                        for u in range(2):
                            kt = 2 * kt2 + u
                            pu_ = P.a[:, u * TC:(u + 1) * TC]
                            mm(O.a[0:64, :], V.a[:, kt, 0:64], pu_, kt == 0, kt == NKT - 1, [V.B, P.B], [O.B], False)
                            mm(Z.a[0:64, :], ones[:, 0:64], pu_, kt == 0, kt == NKT - 1, [cm.B, P.B], [Z.B], u == 1)
                    r = r_r.next()
                    recip(r.a[0:64, :], Z.a[0:64, :], [Z.B], [r.B])
                    ob = ob_r.next()
                    tt("dve", ob.a[0:64, :], O.a[0:64, :], r.a[0:64, :], ALU.mult, [O.B, r.B], [ob.B])
                    S.dma("sp", mixT[512 + 64 * hq:576 + 64 * hq, cs], ob.a[0:64, :], reads=[ob.B])

    def phase_B_odd(i):
        S.barrier()
        arena.reset()
        j = i // 2
        KAr = Ring([arena.tile([SEQ], BF16) for _ in range(2)])
        QAr = Ring([arena.tile([NTOK], BF16) for _ in range(2)])
        Vr = Ring([arena.tile([NKT, 64], BF16) for _ in range(2)])
        Tr = Ring([arena.tile([8, TC], BF16, 16) for _ in range(2)])
        Pr = Ring([arena.tile([2 * TC], BF16) for _ in range(3)])
        SPr = Ring(PP[0:2])
        r_r = Ring([arena.tile([TC], F32) for _ in range(2)])
        ob_r = Ring([arena.tile([TC], BF16) for _ in range(2)])
        Sr = Ring(ps[0:4])
        OZr = Ring([(ps[4], ps[5]), (ps[6], ps[7])])
        for t_ in KAr.items:
            S.dma("pool", t_.a[64:80, :], I("naU"), writes=[t_.B])
        for t_ in QAr.items:
            S.dma("pool", t_.a[64:80, :], I("naV")[:, 0:NTOK], writes=[t_.B])
        for h in range(16):
            KA = KAr.next()
            S.dma("sp", KA.a[0:64, :], kT[64 * h:64 * h + 64, :], writes=[KA.B])
            QA = QAr.next()
            S.dma("sp", QA.a[0:64, :], qT[64 * h:64 * h + 64, :], writes=[QA.B])
            V = Vr.next()
            S.dma("sp", V.a, vtm[:, 64 * h:64 * h + 64].rearrange("(kt p) d -> p kt d", p=128), writes=[V.B])
            T = Tr.next()
            for a_ in range(2):
                for jj in range(8):
                    i0 = 15 - 2 * jj - a_
                    S.dma("pool", T.a[64 * a_:64 * a_ + 64, jj, :].rearrange("k (r q) -> k r q", q=64),
                          I("natab")[j, h, i0:i0 + 8].rearrange("r k q -> k r q"), writes=[T.b[a_ * 8 + jj]])
            for c in range(NCH):
                cs = slice(c * TC, (c + 1) * TC)
                jjs = [jj for jj in range(8) if 0 <= 4 * c - 2 + jj < NKT]
                O, Z = OZr.next()
                assert len(jjs) % 2 == 0
                for n2 in range(len(jjs) // 2):
                    SP = SPr.next()
                    for u in range(2):
                        jj = jjs[2 * n2 + u]
                        gt = 4 * c - 2 + jj
                        ks = slice(gt * 128, (gt + 1) * 128)
                        so = SP.a[:, u * TC:(u + 1) * TC]
                        mm(so, KA.a[0:80, ks], QA.a[0:80, cs], True, False, [KA.B, QA.B], [SP.b[u]], False)
                        mm(so, ident, T.a[:, jj, :], False, True, [cm.B, T.b[jj], T.b[8 + jj]], [SP.b[u]], u == 1)
                    P = Pr.next()
                    act(P.a, SP.a, AF.Exp, SP.b, [P.B])
                    for u in range(2):
                        n = 2 * n2 + u
                        gt = 4 * c - 2 + jjs[n]
                        pu_ = P.a[:, u * TC:(u + 1) * TC]
                        mm(O.a[0:64, :], V.a[:, gt, :], pu_, n == 0, n == len(jjs) - 1, [V.B, P.B], [O.B], False)
                        mm(Z.a[0:64, :], ones[:, 0:64], pu_, n == 0, n == len(jjs) - 1, [cm.B, P.B], [Z.B], u == 1)
                r = r_r.next()
                recip(r.a[0:64, :], Z.a[0:64, :], [Z.B], [r.B])
                ob = ob_r.next()
                tt("dve", ob.a[0:64, :], O.a[0:64, :], r.a[0:64, :], ALU.mult, [O.B, r.B], [ob.B])
                S.dma("sp", mixT[64 * h:64 * h + 64, cs], ob.a[0:64, :], reads=[ob.B])

    def phase_C(i):
        S.barrier()
        arena.reset()
        j = i // 2
        Wo = arena.tile([8, D], BF16, 8)
        load_w(Wo, (I("w_out_ab") if i % 2 == 0 else I("w_out_c"))[j], 8)
        xsr = Ring([arena.tile([8, TC], F32, 8) for _ in range(2)])
        mxr = Ring([arena.tile([8, TC], BF16) for _ in range(2)])
        hr = Ring([arena.tile([8, TC], BF16, 8) for _ in range(2)])
        scr = {"sq": arena.tile([8, TC], BF16, 8), "rt": arena.tile([TC], F32), "psring": Ring(ps[6:8])}
        psr = Ring(ps[0:6])
        for c in range(NCH):
            cs = slice(c * TC, (c + 1) * TC)
            xs = xsr.next()
            S.dma("sp", xs.a, xres[:, cs].rearrange("(k p) t -> p k t", p=128), writes=xs.b)
            mx = mxr.next()
            S.dma("sp", mx.a, mixT[:, cs].rearrange("(k p) t -> p k t", p=128), writes=[mx.B])
            for dc in range(8):
                ds = slice(dc * 128, (dc + 1) * 128)
                p_ = psr.next()
                for kc in range(8):
                    mm(p_.a, Wo.a[:, kc, ds], mx.a[:, kc, :], kc == 0, kc == 7, [Wo.b[kc], mx.B], [p_.B], kc == 7)
                tt("dve", xs.a[:, dc, :], p_.a, xs.a[:, dc, :], ALU.add, [p_.B, xs.b[dc]], [xs.b[dc]])
            S.dma("sp", xres[:, cs].rearrange("(k p) t -> p k t", p=128), xs.a, reads=xs.b)
            h2 = hr.next()
            rmsnorm(xs, "ffn_norm%d" % i, h2, scr)
            S.dma("sp", h2T[:, cs].rearrange("(k p) t -> p k t", p=128), h2.a, reads=h2.b)

    def phase_D(i):
        S.barrier()
        arena.reset()
        Wu = arena.tile([8, 2 * DFF], BF16, 8)
        Wd = arena.tile([NFC, D], BF16, NFC)
        load_w(Wu, I("w_ffn_up")[i], 8)
        load_w(Wd, I("w_ffn_down")[i], NFC)
        h2r = Ring([arena.tile([8, TC + 2], BF16) for _ in range(2)])
        g_t = arena.tile([NFC, TC], BF16, NFC)
        cg_r = Ring([arena.tile([256], F32) for _ in range(2)])
        cv_r = Ring([arena.tile([256], F32) for _ in range(2)])
        sg_r = Ring([arena.tile([256], F32) for _ in range(2)])
        xd_r = Ring([arena.tile([TC], F32) for _ in range(2)])
        psr = Ring(ps[0:8])
        cwo, _ = L.off["conv_w%d" % i]
        cbo, _ = L.off["conv_b%d" % i]

        def cw(jtap, m):
            return small.a[:, cwo + jtap * 44 + m:cwo + jtap * 44 + m + 1]

        def cb(m):
            return small.a[:, cbo + m:cbo + m + 1]

        for c in range(NCH):
            cs = slice(c * TC, (c + 1) * TC)
            h2 = h2r.next()
            lo = max(c * TC - 1, 0)
            hi = min(c * TC + TC + 1, NTOK)
            o0 = lo - (c * TC - 1)
            S.dma("sp", h2.a[:, :, o0:o0 + hi - lo], h2T[:, lo:hi].rearrange("(k p) t -> p k t", p=128), writes=[h2.B])
            if o0 > 0:
                S.op("dve", lambda e, o=h2.a[:, :, 0:1]: e.memset(o, 0.0), writes=[h2.B])
            if hi - lo + o0 < TC + 2:
                S.op("dve", lambda e, o=h2.a[:, :, TC + 1:TC + 2]: e.memset(o, 0.0), writes=[h2.B])
            for fc in range(NFC):
                for sub in range(2):
                    c0 = 256 * sub
                    res = []
                    for part, (ring_) in enumerate((cg_r, cv_r)):
                        m = part * NFC + fc
                        n0 = part * DFF + fc * 128
                        pu = psr.next()
                        for kc in range(8):
                            mm(pu.a[:, 0:258], Wu.a[:, kc, n0:n0 + 128], h2.a[:, kc, c0:c0 + 258], kc == 0, kc == 7,
                               [Wu.b[kc], h2.B], [pu.B], kc == 7)
                        cc = ring_.next()
                        act(cc.a, pu.a[:, 1:257], AF.Identity, [pu.B, small.B], [cc.B], bias=cb(m), scale=cw(1, m))
                        stt("dve", cc.a, pu.a[:, 0:256], cw(0, m), cc.a, ALU.mult, ALU.add, [pu.B, cc.B, small.B], [cc.B])
                        stt("dve", cc.a, pu.a[:, 2:258], cw(2, m), cc.a, ALU.mult, ALU.add, [pu.B, cc.B, small.B], [cc.B])
                        res.append(cc)
                    sg_ = sg_r.next()
                    act(sg_.a, res[0].a, AF.Silu, [res[0].B], [sg_.B])
                    tt("pool", g_t.a[:, fc, c0:c0 + 256], sg_.a, res[1].a, ALU.mult, [sg_.B, res[1].B], [g_t.b[fc]])
            for dc in range(8):
                ds = slice(dc * 128, (dc + 1) * 128)
                xd = xd_r.next()
                S.dma("sp", xd.a, xres[ds, cs], writes=[xd.B])
                p_ = psr.next()
                for fc in range(NFC):
                    mm(p_.a, Wd.a[:, fc, ds], g_t.a[:, fc, :], fc == 0, fc == NFC - 1, [Wd.b[fc], g_t.b[fc]], [p_.B], fc == NFC - 1)
                tt("dve", xd.a, p_.a, xd.a, ALU.add, [p_.B, xd.B], [xd.B])
                S.dma("sp", xres[ds, cs], xd.a, reads=[xd.B])

    first = True
    for i in layers:
        if stop == "P":
            break
        phase_EA(i, first)
        first = False
        if stop == "EA":
            break
        if i % 2 == 0:
            phase_B_even(i)
        else:
            phase_B_odd(i)
        if stop == "B":
            break
        phase_C(i)
        if dump and i == layers[-1]:
            S.barrier()
            for c in range(NCH):
                S.dma("sp", dbgT[:, c * TC:(c + 1) * TC], xres[:, c * TC:(c + 1) * TC])
        if stop == "C":
            break
        phase_D(i)
    if final and stop is None:
        phase_EA(layers[-1] + 1 if layers[-1] + 1 < DEPTH else DEPTH, False)
    if dump:
        S.barrier()
        for c in range(NCH):
            cs = slice(c * TC, (c + 1) * TC)
            S.dma("sp", outT[:, cs], xres[:, cs])
    S.barrier()
    S.emit()
    S.used_inputs = list(_ins.keys())
    return nc, S


_CACHE = {}


def host_consts(inp):
    U, Vm = na_rowmask()
    return {
        "small": pack_small(inp),
        "cmat": const_mats(),
        "ropeA": rope_tables()[0],
        "ropeB": rope_tables()[1],
        "natab": np.stack([na_tables(np.asarray(inp["c_rel_bias"][j])) for j in range(2)]),
        "naU": U,
        "naV": Vm,
    }


WNAMES = ("w_in_ab", "w_out_ab", "w_in_c", "w_out_c", "w_ffn_up", "w_ffn_down", "w_ple_gate", "w_ple_proj")


def kernel(**inp):
    inp = {k: np.asarray(v) for k, v in inp.items()}
    x = inp["x"]
    p = inp["p"]
    B = x.shape[0]
    if "nc" not in _CACHE:
        _CACHE["nc"] = build()[0]
    nc = _CACHE["nc"]
    consts = host_consts(inp)
    in_maps = []
    for b in range(B):
        m = dict(consts)
        m["xT"] = np.ascontiguousarray(x[b].T)
        m["pT"] = np.ascontiguousarray(np.transpose(p[:, b], (0, 2, 1)))
        for w in WNAMES:
            m[w] = np.ascontiguousarray(inp[w], dtype=np.float32)
        in_maps.append(m)
    res = run_bass_kernel_spmd(nc, in_maps, core_ids=list(range(B)))
    out = np.stack([np.ascontiguousarray(res.results[b]["outT"].T) for b in range(B)])
    return out.astype(np.float32)
```

```python
import math
import numpy as np
import concourse.bass as bass
import concourse.mybir as mybir
from concourse.bass_utils import run_bass_kernel_spmd

F32 = mybir.dt.float32
BF16 = mybir.dt.bfloat16
U8 = mybir.dt.uint8
AF = mybir.ActivationFunctionType
ALU = mybir.AluOpType
AX = mybir.AxisListType

D = 1024
SEQ = 8192
DEPTH = 4
DFF = 2816
NFC = DFF // 128
PLE = 256
EPS = 1e-6
NEG = -30000.0
TC = 512
DBG = {}
NKT = SEQ // 128


class Buf:
    __slots__ = ("w", "r", "excl")

    def __init__(self):
        self.w = None
        self.r = []
        self.excl = False


class Sched:
    ENG = ("pe", "dve", "act", "pool", "sp")
    NPOOL = 8

    def __init__(self, nc):
        self.nc = nc
        self.q = {e: [] for e in self.ENG}
        self.sem = {e: nc.alloc_semaphore("s_" + e) for e in self.ENG}
        self.cnt = {e: 0 for e in self.ENG}
        self.known = {e: {} for e in self.ENG}
        self.dpool = {qn: [[nc.alloc_semaphore("d_%s%d" % (qn, i)), 0] for i in range(self.NPOOL)]
                      for qn in ("sp", "pool")}
        self.dpi = {"sp": 0, "pool": 0}
        self.ninst = 0

    def _wait(self, eng, tok):
        sem, val = tok
        k = self.known[eng]
        key = sem.num
        if k.get(key, 0) >= val:
            return
        k[key] = val
        self.q[eng].append(lambda e, sem=sem, val=val: e.wait_ge(sem, val))
        self.ninst += 1

    def _deps(self, eng, reads, writes):
        own = self.sem[eng].num
        lim = 1 << 60 if eng == "pe" else self.cnt[eng] - 3

        def need(t):
            return t[0].num != own or t[1] > lim

        for b in reads:
            if b.w is not None and need(b.w):
                self._wait(eng, b.w)
        for b in writes:
            if b.w is not None and need(b.w):
                self._wait(eng, b.w)
            for t in b.r:
                if need(t):
                    self._wait(eng, t)

    def _update(self, tok, reads, writes):
        for b in writes:
            b.w = tok
            b.r = []
        for b in reads:
            b.r.append(tok)
            if len(b.r) > 24:
                b.r = b.r[-24:]

    def op(self, eng, fn, reads=(), writes=(), inc=True):
        if eng == "pool" and DBG.get("nopool"):
            eng = "dve"
        if any(b.excl for b in reads):
            writes = list(writes) + [b for b in reads if b.excl]
            reads = [b for b in reads if not b.excl]
        self._deps(eng, reads, writes)
        sem = self.sem[eng]
        if inc:
            self.cnt[eng] += 1
            tok = (sem, self.cnt[eng])
            self.q[eng].append(lambda e, fn=fn, sem=sem: fn(e).then_inc(sem, 1))
        else:
            tok = (sem, self.cnt[eng] + 1)
            self.q[eng].append(lambda e, fn=fn: fn(e))
        self.ninst += 1
        self._update(tok, reads, writes)
        return tok

    def dma(self, qn, out, in_, reads=(), writes=(), slow=False):
        self._deps(qn, reads, writes)
        pool = self.dpool[qn]
        i = self.dpi[qn]
        self.dpi[qn] = (i + 1) % len(pool)
        ent = pool[i]
        if ent[1] > 0:
            self._wait(qn, (ent[0], 16 * ent[1]))
        ent[1] += 1
        sem = ent[0]
        tok = (sem, 16 * ent[1])
        if slow:
            self.q[qn].append(lambda e, out=out, in_=in_, sem=sem: e.dma_start(out=out, in_=in_, allow_slow_non_contiguous=True).then_inc(sem, 16))
        else:
            self.q[qn].append(lambda e, out=out, in_=in_, sem=sem: e.dma_start(out=out, in_=in_).then_inc(sem, 16))
        self.ninst += 1
        self._update(tok, reads, writes)
        return tok

    def all_tokens(self):
        toks = []
        for e in self.ENG:
            if self.cnt[e] > 0:
                toks.append((self.sem[e], self.cnt[e]))
        for qn in self.dpool:
            for ent in self.dpool[qn]:
                if ent[1] > 0:
                    toks.append((ent[0], 16 * ent[1]))
        return toks

    def barrier(self, engines=None):
        toks = self.all_tokens()
        for e in (engines or self.ENG):
            for t in toks:
                self._wait(e, t)

    def emit(self):
        q = self.q
        with self.nc.Block() as block:
            block.tensor(lambda e: [f(e) for f in q["pe"]])
            block.vector(lambda e: [f(e) for f in q["dve"]])
            block.scalar(lambda e: [f(e) for f in q["act"]])
            block.gpsimd(lambda e: [f(e) for f in q["pool"]])
            block.sync(lambda e: [f(e) for f in q["sp"]])


class Ring:
    def __init__(self, items):
        self.items = items
        self.i = 0

    def next(self):
        it = self.items[self.i]
        self.i = (self.i + 1) % len(self.items)
        return it


class Tile:
    def __init__(self, ap, nb=1):
        self.a = ap
        self.b = [Buf() for _ in range(nb)]

    @property
    def B(self):
        return self.b[0]


class Arena:
    def __init__(self, nc, nbytes):
        self.t = nc.alloc_sbuf_tensor("arena", [128, nbytes], U8)
        self.n = nbytes
        self.off = 0

    def reset(self):
        self.off = 0

    def tile(self, shape, dtype, nb=1):
        es = 2 if dtype == BF16 else 4
        tot = 1
        for s in shape:
            tot *= s
        n = (tot * es + 63) // 64 * 64
        assert self.off + n <= self.n, ("arena overflow", self.off, n, self.n)
        ap = self.t[:, self.off:self.off + tot * es].bitcast(dtype)
        self.off += n
        if len(shape) == 2:
            ap = ap.rearrange("p (a b) -> p a b", b=shape[1])
        elif len(shape) == 3:
            ap = ap.rearrange("p (a b c) -> p a b c", b=shape[1], c=shape[2])
        return Tile(ap, nb)


class SmallLayout:
    def __init__(self):
        self.off = {}
        self.n = 0

    def add(self, name, width):
        self.off[name] = (self.n, width)
        self.n += width


def small_layout():
    L = SmallLayout()
    for i in range(DEPTH):
        L.add("attn_norm%d" % i, 8)
        L.add("ffn_norm%d" % i, 8)
        L.add("ple_norm%d" % i, 8)
        L.add("conv_w%d" % i, 3 * 44)
        L.add("conv_b%d" % i, 44)
    L.add("final_norm", 8)
    for j in range(2):
        L.add("subln%d" % j, 1)
        L.add("qnorm%d" % j, 1)
        L.add("knorm%d" % j, 1)
        for nm in ("lq1", "lk1", "lq2", "lk2"):
            L.add("%s_%d" % (nm, j), 64)
    return L


def fm(v):
    return np.ascontiguousarray(v.reshape(-1, 128).T)


def pack_small(inp):
    L = small_layout()
    a = np.zeros((128, L.n), np.float32)

    def put(name, arr):
        o, w = L.off[name]
        assert arr.shape == (128, w), (name, arr.shape, w)
        a[:, o:o + w] = arr

    for i in range(DEPTH):
        put("attn_norm%d" % i, fm(inp["attn_norm"][i]))
        put("ffn_norm%d" % i, fm(inp["ffn_norm"][i]))
        put("ple_norm%d" % i, fm(inp["ple_norm"][i]))
        cw = inp["ffn_conv_w"][i]
        put("conv_w%d" % i, np.concatenate([fm(cw[j]) for j in range(3)], axis=1))
        put("conv_b%d" % i, fm(inp["ffn_conv_b"][i]))
    put("final_norm", fm(inp["final_norm"]))
    for j in range(2):
        put("subln%d" % j, inp["a_subln"][j].reshape(128, 1))
        put("qnorm%d" % j, np.tile(inp["b_q_norm"][j], 2).reshape(128, 1))
        put("knorm%d" % j, np.tile(inp["b_k_norm"][j], 2).reshape(128, 1))
        for nm, key in (("lq1", "lambda_q1"), ("lk1", "lambda_k1"), ("lq2", "lambda_q2"), ("lk2", "lambda_k2")):
            put("%s_%d" % (nm, j), np.broadcast_to(inp[key][j][None, :], (128, 64)))
    return a


def const_mats():
    m = np.zeros((128, 5, 128), np.float32)
    m[:, 0, :] = 1.0
    m[0:64, 1, 0:64] = 1.0
    m[64:128, 1, 64:128] = 1.0
    m[:, 2, :] = np.eye(128, dtype=np.float32)
    for mo in range(128):
        f = mo % 64
        base = mo - f
        if f < 8:
            m[base + f + 8, 3, mo] = 1.0
        elif f < 16:
            m[base + f - 8, 3, mo] = 1.0
        g = f % 32
        gb = f - g
        if g < 16:
            m[base + gb + g + 16, 4, mo] = 1.0
        else:
            m[base + gb + g - 16, 4, mo] = 1.0
    return m


def rope_tables():
    pos = np.arange(SEQ, dtype=np.float32)
    row = (np.arange(SEQ) // 64).astype(np.float32)
    col = (np.arange(SEQ) % 64).astype(np.float32)
    invA = (np.float32(500000.0) ** (-np.arange(8, dtype=np.float32) / np.float32(8))).astype(np.float32)
    invB = (np.float32(10000.0) ** (-np.arange(16, dtype=np.float32) / np.float32(16))).astype(np.float32)
    ra = np.zeros((32, SEQ), np.float32)
    for f in range(16):
        ang = (pos * invA[f % 8]).astype(np.float32)
        ra[f] = np.cos(ang)
        ra[16 + f] = -np.sin(ang) if f < 8 else np.sin(ang)
    rb = np.zeros((128, SEQ), np.float32)
    for f in range(64):
        g = f % 32
        src = row if f < 32 else col
        ang = (src * invB[g % 16]).astype(np.float32)
        rb[f] = np.cos(ang)
        rb[64 + f] = -np.sin(ang) if g < 16 else np.sin(ang)
    return ra, rb


def na_tables(bias):
    kc = np.arange(64)[:, None]
    qc = np.arange(64)[None, :]
    cs = np.clip(qc - 8, 0, 48)
    colv = (kc >= cs) & (kc < cs + 16)
    ci = np.clip(kc - qc + 15, 0, 30)
    out = np.zeros((16, 23, 64, 64), np.float32)
    for i in range(23):
        dr = 11 - i
        for h in range(16):
            if -7 <= dr <= 7:
                v = bias[h, dr + 7][ci]
            else:
                v = np.zeros((64, 64), np.float32)
            out[h, i] = np.where(colv, v, np.float32(NEG))
    return out


def na_rowmask():
    U = np.zeros((16, SEQ), np.float32)
    krow = np.arange(SEQ) // 64
    U[krow % 16, np.arange(SEQ)] = 1.0
    Vm = np.full((16, SEQ), NEG, np.float32)
    for r in range(128):
        c = r // 8
        rs = min(max(r - 4, 0), 120)
        for R in range(8 * c - 4, 8 * c + 12):
            if 0 <= R < 128 and rs <= R < rs + 8:
                Vm[R % 16, r * 64:(r + 1) * 64] = 0.0
    return U, Vm


def build(layers=(0, 1, 2, 3), final=True, ntok=SEQ, dump=False, stop=None):
    NTOK = ntok
    NCH = NTOK // TC
    nc = bass.Bass("TRN2", target_bir_lowering=False)
    L = small_layout()

    def din(name, shape, dt=F32):
        return nc.dram_tensor(name, list(shape), dt, kind="ExternalInput").ap()

    def dscr(name, shape, dt):
        return nc.dram_tensor(name, list(shape), dt, kind="Internal").ap()

    IN_SPECS = {
        "xT": [D, NTOK],
        "pT": [DEPTH, PLE, NTOK],
        "small": [128, L.n],
        "cmat": [128, 5, 128],
        "ropeA": [32, SEQ],
        "ropeB": [128, SEQ],
        "natab": [2, 16, 23, 64, 64],
        "naU": [16, SEQ],
        "naV": [16, SEQ],
        "w_in_ab": [2, D, 2304],
        "w_out_ab": [2, D, D],
        "w_in_c": [2, D, 3072],
        "w_out_c": [2, D, D],
        "w_ffn_up": [DEPTH, D, 2 * DFF],
        "w_ffn_down": [DEPTH, DFF, D],
        "w_ple_gate": [DEPTH, D, D],
        "w_ple_proj": [DEPTH, PLE, D],
    }
    _ins = {}

    def I(name):
        if name not in _ins:
            _ins[name] = din(name, IN_SPECS[name])
        return _ins[name]

    outT = nc.dram_tensor("outT", [D, NTOK], F32, kind="ExternalOutput").ap()
    dbgT = nc.dram_tensor("dbgT", [D, NTOK], F32, kind="ExternalOutput").ap() if dump else None

    xres = dscr("xres", [D, NTOK], F32)
    qT = dscr("qT", [D, NTOK], BF16)
    kT = dscr("kT", [D, SEQ], BF16)
    vtm = dscr("vtm", [SEQ, D], BF16)
    mixT = dscr("mixT", [D, NTOK], BF16)
    h2T = dscr("h2T", [D, NTOK], BF16)

    S = Sched(nc)

    small = Tile(nc.alloc_sbuf_tensor("small_sb", [128, L.n], F32)[:])
    cm = Tile(nc.alloc_sbuf_tensor("cmat_sb", [128, 5, 128], BF16)[:])
    der = Tile(nc.alloc_sbuf_tensor("derived", [128, 16], F32)[:])
    zero = Tile(nc.alloc_sbuf_tensor("zero", [128, 8, 1], BF16)[:])
    pp_h = [nc.alloc_psum_tensor("pp%d" % i, [128, 1024], F32) for i in range(4)]
    ps = []
    for k_ in range(4):
        for u_ in range(2):
            t_ = Tile(pp_h[k_][:, u_ * 512:(u_ + 1) * 512])
            t_.B.excl = True
            ps.append(t_)
    PP = []
    for k_ in range(4):
        t_ = Tile(pp_h[k_][:])
        t_.b = [ps[2 * k_].B, ps[2 * k_ + 1].B]
        PP.append(t_)
    lamtmp = [Tile(nc.alloc_sbuf_tensor("lamtmp%d" % j, [128, 64], F32)[:]) for j in range(2)]
    arena = Arena(nc, nc.sbuf_bytes_remaining - 256)

    ones = cm.a[:, 0, :]
    blk64 = cm.a[:, 1, :]
    ident = cm.a[:, 2, :]
    RAT = cm.a[:, 3, :]
    RBT = cm.a[:, 4, :]

    def sm(name, j=None):
        o, w = L.off[name]
        if j is None:
            return small.a[:, o:o + w]
        return small.a[:, o + j:o + j + 1]

    def mm(out, lhsT, rhs, start, stop, reads, writes, inc):
        return S.op("pe", lambda e: e.matmul(out, lhsT, rhs, start=start, stop=stop), reads, writes, inc)

    def act(out, in_, func, reads, writes, bias=0.0, scale=1.0):
        return S.op("act", lambda e: e.activation(out=out, in_=in_, func=func, bias=bias, scale=scale), reads, writes)

    def tt(eng, out, in0, in1, op, reads, writes):
        return S.op(eng, lambda e: e.tensor_tensor(out=out, in0=in0, in1=in1, op=op), reads, writes)

    def stt(eng, out, in0, scalar, in1, op0, op1, reads, writes):
        return S.op(eng, lambda e: e.scalar_tensor_tensor(out=out, in0=in0, scalar=scalar, in1=in1, op0=op0, op1=op1),
                    reads, writes)

    def tcopy(eng, out, in_, reads, writes):
        return S.op(eng, lambda e: e.tensor_copy(out=out, in_=in_), reads, writes)

    def recip(out, in_, reads, writes):
        return S.op("dve", lambda e: e.reciprocal(out=out, in_=in_), reads, writes)

    S.dma("sp", small.a, I("small"), writes=[small.B])
    S.dma("pool", cm.a, I("cmat"), writes=[cm.B])
    S.op("dve", lambda e: e.memset(zero.a, 0.0), writes=[zero.B])
    for j in range(2):
        lam_init = 0.8 - 0.6 * math.exp(-0.3 * (2 * j))
        tmp = lamtmp[j]
        for n, (a, b) in enumerate((("lq1", "lk1"), ("lq2", "lk2"))):
            tt("dve", tmp.a, sm("%s_%d" % (a, j)), sm("%s_%d" % (b, j)), ALU.mult, [small.B], [tmp.B])
            S.op("dve", lambda e, o=der.a[:, 8 + 4 * j + n:9 + 4 * j + n], t=tmp.a: e.reduce_sum(out=o, in_=t, axis=AX.X),
                 [tmp.B], [der.B])
        act(der.a[:, 8 + 4 * j:10 + 4 * j], der.a[:, 8 + 4 * j:10 + 4 * j], AF.Exp, [der.B], [der.B])
        stt("dve", der.a[:, 4 * j:4 * j + 1], der.a[:, 9 + 4 * j:10 + 4 * j], -lam_init, der.a[:, 8 + 4 * j:9 + 4 * j],
            ALU.add, ALU.subtract, [der.B], [der.B])
        S.op("dve", lambda e, o=der.a[:, 4 * j + 1:4 * j + 2], i_=sm("subln%d" % j), c=1.0 - lam_init:
             e.tensor_scalar(out=o, in0=i_, scalar1=c, scalar2=None, op0=ALU.mult), [small.B, der.B], [der.B])

    def rmsnorm(xs, gname, h, scr, nfeat_inv=1.0 / D):
        sq, rt = scr["sq"], scr["rt"]
        pss = scr["psring"].next()
        for kc in range(8):
            act(sq.a[:, kc, :], xs.a[:, kc, :], AF.Square, [xs.b[kc]], [sq.b[kc]])
        for kc in range(8):
            mm(pss.a, ones, sq.a[:, kc, :], kc == 0, kc == 7, [cm.B, sq.b[kc]], [pss.B], kc == 7)
        act(rt.a, pss.a, AF.Sqrt, [pss.B], [rt.B], bias=EPS, scale=nfeat_inv)
        recip(rt.a, rt.a, [rt.B], [rt.B])
        for kc in range(8):
            stt("dve", h.a[:, kc, :], xs.a[:, kc, :], sm(gname, kc), rt.a, ALU.mult, ALU.mult,
                [xs.b[kc], rt.B, small.B], [h.b[kc]])

    def load_w(dst, src_rows, nk, qn="pool"):
        for kc in range(nk):
            S.dma(qn, dst.a[:, kc, :], src_rows[kc * 128:(kc + 1) * 128, :], writes=[dst.b[kc]])

    def phase_EA(i, first):
        S.barrier()
        arena.reset()
        do_ple = not first
        do_proj = i < DEPTH and i in layers
        do_final = (i == DEPTH)
        even = (i % 2 == 0)
        j = i // 2
        xsr = Ring([arena.tile([8, TC], F32, 8) for _ in range(2)])
        scr = {"sq": arena.tile([8, TC], BF16, 8), "rt": arena.tile([TC], F32), "psring": Ring(ps[6:8])}
        hr = Ring([arena.tile([8, TC], BF16, 8) for _ in range(2)])
        psr = Ring(ps[0:6])
        if do_ple:
            Wg = arena.tile([8, D], BF16, 8)
            Wp = arena.tile([2, D], BF16, 2)
            load_w(Wg, I("w_ple_gate")[i - 1], 8)
            load_w(Wp, I("w_ple_proj")[i - 1], 2)
            pr = Ring([arena.tile([2, TC], BF16) for _ in range(2)])
            gate_r = Ring([arena.tile([TC], F32) for _ in range(2)])
        if do_final:
            hf = arena.tile([8, TC], F32, 8)
        if do_proj:
            NW = 2304 if even else 3072
            Win = arena.tile([8, NW], BF16, 8)
            load_w(Win, (I("w_in_ab") if even else I("w_in_c"))[j], 8)
            if even:
                tabr = Ring([arena.tile([4, TC], F32, 4) for _ in range(2)])
                for t_ in tabr.items:
                    S.op("pool", lambda e, o=t_.a[:, 0, :]: e.memset(o, 1.0), writes=t_.b)
                    S.op("pool", lambda e, o=t_.a[:, 1, :]: e.memset(o, 0.0), writes=t_.b)
                qf_r = Ring([arena.tile([TC], F32) for _ in range(2)])
                qb_r = Ring([arena.tile([TC], BF16) for _ in range(2)])
                t1_r = Ring([arena.tile([TC], F32) for _ in range(2)])
                t2_r = Ring([arena.tile([TC], F32) for _ in range(2)])
                sqb_r = Ring([arena.tile([TC], BF16) for _ in range(2)])
                rs_r = Ring([arena.tile([TC], F32) for _ in range(2)])
            ob_r = Ring([arena.tile([TC], BF16) for _ in range(3)])
            vo_r = Ring([arena.tile([TC], BF16) for _ in range(2)])
        src = I("xT") if first else xres
        steps = DBG.get("ea_steps", 99)

        for c in range(min(NCH, DBG.get("ea_chunks", NCH))):
            cs = slice(c * TC, (c + 1) * TC)
            xs = xsr.next()
            S.dma("sp", xs.a, src[:, cs].rearrange("(k p) t -> p k t", p=128), writes=xs.b)
            if do_proj and even:
                tab = tabr.next()
                for n_, pb in enumerate((0, 64)):
                    S.dma("sp", tab.a[pb:pb + 16, 0:2, :], I("ropeA")[:, cs].rearrange("(f p) t -> p f t", p=16), writes=[tab.b[n_]])
                    S.dma("sp", tab.a[pb:pb + 64, 2:4, :], I("ropeB")[:, cs].rearrange("(f p) t -> p f t", p=64), writes=[tab.b[2 + n_]])
            if do_ple:
                pch = pr.next()
                S.dma("pool", pch.a, I("pT")[i - 1][:, cs].rearrange("(k p) t -> p k t", p=128), writes=[pch.B])
                h3 = hr.next()
                rmsnorm(xs, "ple_norm%d" % (i - 1), h3, scr)
                for dc in range(8):
                    ds = slice(dc * 128, (dc + 1) * 128)
                    pg = psr.next()
                    for kc in range(8):
                        mm(pg.a, Wg.a[:, kc, ds], h3.a[:, kc, :], kc == 0, kc == 7, [Wg.b[kc], h3.b[kc]], [pg.B], kc == 7)
                    gate = gate_r.next()
                    act(gate.a, pg.a, AF.Sigmoid, [pg.B], [gate.B])
                    pp = psr.next()
                    for k2 in range(2):
                        mm(pp.a, Wp.a[:, k2, ds], pch.a[:, k2, :], k2 == 0, k2 == 1, [Wp.b[k2], pch.B], [pp.B], k2 == 1)
                    tt("dve", gate.a, pp.a, gate.a, ALU.mult, [pp.B, gate.B], [gate.B])
                    tt("dve", xs.a[:, dc, :], xs.a[:, dc, :], gate.a, ALU.add, [xs.b[dc], gate.B], [xs.b[dc]])
            if do_final:
                rmsnorm(xs, "final_norm", hf, scr)
                S.dma("sp", outT[:, cs].rearrange("(k p) t -> p k t", p=128), hf.a, reads=hf.b)
                continue
            if do_ple or first:
                S.dma("sp", xres[:, cs].rearrange("(k p) t -> p k t", p=128), xs.a, reads=xs.b)
            if not do_proj or steps <= 1:
                continue
            h = hr.next()
            rmsnorm(xs, "attn_norm%d" % i, h, scr)
            if steps <= 2:
                continue

            def proj_fm(n0):
                p_ = psr.next()
                for kc in range(8):
                    mm(p_.a, Win.a[:, kc, n0:n0 + 128], h.a[:, kc, :], kc == 0, kc == 7, [Win.b[kc], h.b[kc]], [p_.B], kc == 7)
                return p_

            def rope_finish(qsrc_f32, qsrc_b, ci, RT, dst_rows, dst):
                prot = psr.next()
                mm(prot.a, RT, qsrc_b.a, True, True, [cm.B, qsrc_b.B], [prot.B], True)
                t1 = t1_r.next()
                tt("pool", t1.a, qsrc_f32.a, tab.a[:, ci, :], ALU.mult, [qsrc_f32.B] + tab.b, [t1.B])
                t2 = t2_r.next()
                tt("dve", t2.a, prot.a, tab.a[:, ci + 1, :], ALU.mult, [prot.B] + tab.b, [t2.B])
                ob = ob_r.next()
                tt("pool", ob.a, t1.a, t2.a, ALU.add, [t1.B, t2.B], [ob.B])
                if DBG.get("ea_sub", 99) <= 3:
                    return
                S.dma("sp", dst[dst_rows, cs], ob.a, reads=[ob.B])

            if even:
                sub = DBG.get("ea_sub", 99)
                for which, n_base, dst in ((0, 0, qT), (1, 512, kT)):
                    for hA in range(4):
                        p_ = proj_fm(n_base + hA * 128)
                        if sub <= 1:
                            continue
                        qf = qf_r.next()
                        act(qf.a, p_.a, AF.Identity, [p_.B], [qf.B])
                        qb = qb_r.next()
                        tcopy("dve", qb.a, p_.a, [p_.B], [qb.B])
                        if sub <= 2:
                            continue
                        rope_finish(qf, qb, 0, RAT, slice(hA * 128, (hA + 1) * 128), dst)
                if steps <= 3:
                    continue
                for n0, gname, rows, dst in ([(1536 + m * 128, "qnorm%d" % j, slice(512 + m * 128, 640 + m * 128), qT) for m in range(4)]
                                             + [(2048, "knorm%d" % j, slice(512, 640), kT)]):
                    p_ = proj_fm(n0)
                    qf = qf_r.next()
                    act(qf.a, p_.a, AF.Identity, [p_.B], [qf.B])
                    sqb = sqb_r.next()
                    act(sqb.a, p_.a, AF.Square, [p_.B], [sqb.B])
                    pss = psr.next()
                    mm(pss.a, blk64, sqb.a, True, True, [cm.B, sqb.B], [pss.B], True)
                    rs = rs_r.next()
                    act(rs.a, pss.a, AF.Sqrt, [pss.B], [rs.B], bias=EPS, scale=1.0 / 64)
                    recip(rs.a, rs.a, [rs.B], [rs.B])
                    stt("dve", qf.a, qf.a, sm(gname), rs.a, ALU.mult, ALU.mult, [qf.B, rs.B, small.B], [qf.B])
                    qb = qb_r.next()
                    tcopy("pool", qb.a, qf.a, [qf.B], [qb.B])
                    rope_finish(qf, qb, 2, RBT, rows, dst)
                if steps <= 4:
                    continue
                for tsub in range(4):
                    tsl = slice(tsub * 128, (tsub + 1) * 128)
                    t0 = c * TC + tsub * 128
                    for (w0, wn, v0) in ((1024, 512, 0), (2176, 128, 512)):
                        p_ = psr.next()
                        for kc in range(8):
                            mm(p_.a[:, 0:wn], h.a[:, kc, tsl], Win.a[:, kc, w0:w0 + wn], kc == 0, kc == 7,
                               [Win.b[kc], h.b[kc]], [p_.B], kc == 7)
                        vo = vo_r.next()
                        S.op("act", lambda e, o=vo.a[:, 0:wn], i_=p_.a[:, 0:wn]: e.copy(out=o, in_=i_), [p_.B], [vo.B])
                        S.dma("sp", vtm[t0:t0 + 128, v0:v0 + wn], vo.a[:, 0:wn], reads=[vo.B])
            else:
                for m in range(8):
                    p_ = proj_fm(m * 128)
                    ob = ob_r.next()
                    act(ob.a, p_.a, AF.Identity, [p_.B], [ob.B], scale=0.125)
                    S.dma("sp", qT[m * 128:(m + 1) * 128, cs], ob.a, reads=[ob.B])
                for m in range(8):
                    p_ = proj_fm(1024 + m * 128)
                    ob = ob_r.next()
                    tcopy("dve", ob.a, p_.a, [p_.B], [ob.B])
                    S.dma("sp", kT[m * 128:(m + 1) * 128, cs], ob.a, reads=[ob.B])
                for tsub in range(4):
                    tsl = slice(tsub * 128, (tsub + 1) * 128)
                    t0 = c * TC + tsub * 128
                    for half in range(2):
                        p_ = psr.next()
                        w0 = 2048 + half * 512
                        for kc in range(8):
                            mm(p_.a, h.a[:, kc, tsl], Win.a[:, kc, w0:w0 + 512], kc == 0, kc == 7,
                               [Win.b[kc], h.b[kc]], [p_.B], kc == 7)
                        vo = vo_r.next()
                        if half == 0:
                            S.op("act", lambda e, o=vo.a, i_=p_.a: e.copy(out=o, in_=i_), [p_.B], [vo.B])
                        else:
                            tcopy("dve", vo.a, p_.a, [p_.B], [vo.B])
                        S.dma("sp", vtm[t0:t0 + 128, half * 512:(half + 1) * 512], vo.a, reads=[vo.B])

    def attn_pipe(npairs, SPr, Pr, s_emit, o_emit, scale):
        SP = SPr.next()
        s_emit(0, SP)
        for n in range(npairs):
            P = Pr.next()
            act(P.a, SP.a, AF.Exp, SP.b, [P.B], scale=scale)
            if n + 1 < npairs:
                SP = SPr.next()
                s_emit(n + 1, SP)
            o_emit(n, P)

    def phase_B_even(i):
        S.barrier()
        arena.reset()
        j = i // 2
        KTr = Ring([arena.tile([SEQ], BF16) for _ in range(2)])
        Vr = Ring([arena.tile([NKT, 128], BF16) for _ in range(2)])
        Qr = Ring([arena.tile([TC], BF16) for _ in range(2)])
        Pr = Ring([arena.tile([2 * TC], BF16) for _ in range(3)])
        SPr = Ring(PP[0:2])
        a_r = Ring([arena.tile([TC], F32) for _ in range(2)])
        r_r = Ring([arena.tile([TC], F32) for _ in range(2)])
        d_t = arena.tile([TC], F32)
        sq_t = arena.tile([TC], BF16)
        rs_t = arena.tile([TC], F32)
        ob_r = Ring([arena.tile([TC], BF16) for _ in range(2)])
        Sr = Ring(ps[0:4])
        OZr = Ring([(ps[4], ps[5]), (ps[6], ps[7])])
        neglam = der.a[:, 4 * j:4 * j + 1]
        sg = der.a[:, 4 * j + 1:4 * j + 2]

        for hA in range(4):
            KT = KTr.next()
            S.dma("sp", KT.a, kT[hA * 128:(hA + 1) * 128, :], writes=[KT.B])
            V = Vr.next()
            S.dma("sp", V.a, vtm[:, hA * 128:(hA + 1) * 128].rearrange("(kt p) d -> p kt d", p=128), writes=[V.B])
            for qc in range(NCH):
                cs = slice(qc * TC, (qc + 1) * TC)
                Q = Qr.next()
                S.dma("sp", Q.a, qT[hA * 128:(hA + 1) * 128, cs], writes=[Q.B])
                aj = []
                for jm in range(2):
                    rows = slice(64 * jm, 64 * jm + 64)
                    O, Z = OZr.next()
                    def s_emit(kt2, SP, KT=KT, Q=Q, rows=rows):
                        for u in range(2):
                            kt = 2 * kt2 + u
                            ks = slice(kt * 128, (kt + 1) * 128)
                            mm(SP.a[:, u * TC:(u + 1) * TC], KT.a[rows, ks], Q.a[rows, :], True, True, [KT.B, Q.B], [SP.b[u]], u == 1)

                    def o_emit(kt2, P, V=V, O=O, Z=Z):
                        for u in range(2):
                            kt = 2 * kt2 + u
                            pu_ = P.a[:, u * TC:(u + 1) * TC]
                            mm(O.a, V.a[:, kt, :], pu_, kt == 0, kt == NKT - 1, [V.B, P.B], [O.B], False)
                            mm(Z.a, ones, pu_, kt == 0, kt == NKT - 1, [cm.B, P.B], [Z.B], u == 1)

                    attn_pipe(NKT // 2, SPr, Pr, s_emit, o_emit, 0.125)
                    r = r_r.next()
                    recip(r.a, Z.a, [Z.B], [r.B])
                    a_ = a_r.next()
                    tt("dve", a_.a, O.a, r.a, ALU.mult, [O.B, r.B], [a_.B])
                    aj.append(a_)
                stt("dve", d_t.a, aj[1].a, neglam, aj[0].a, ALU.mult, ALU.add, [aj[0].B, aj[1].B, der.B], [d_t.B])
                act(sq_t.a, d_t.a, AF.Square, [d_t.B], [sq_t.B])
                pss = Sr.next()
                mm(pss.a, ones, sq_t.a, True, True, [cm.B, sq_t.B], [pss.B], True)
                act(rs_t.a, pss.a, AF.Sqrt, [pss.B], [rs_t.B], bias=EPS, scale=1.0 / 128)
                recip(rs_t.a, rs_t.a, [rs_t.B], [rs_t.B])
                ob = ob_r.next()
                stt("dve", ob.a, d_t.a, sg, rs_t.a, ALU.mult, ALU.mult, [d_t.B, rs_t.B, der.B], [ob.B])
                S.dma("sp", mixT[hA * 128:(hA + 1) * 128, cs], ob.a, reads=[ob.B])
        for g in range(2):
            KT = KTr.next()
            S.dma("sp", KT.a[0:64, :], kT[512 + 64 * g:576 + 64 * g, :], writes=[KT.B])
            V = Vr.next()
            S.dma("sp", V.a[:, :, 0:64], vtm[:, 512 + 64 * g:576 + 64 * g].rearrange("(kt p) d -> p kt d", p=128), writes=[V.B])
            for r4 in range(4):
                hq = 4 * g + r4
                for qc in range(NCH):
                    cs = slice(qc * TC, (qc + 1) * TC)
                    Q = Qr.next()
                    S.dma("sp", Q.a[0:64, :], qT[512 + 64 * hq:576 + 64 * hq, cs], writes=[Q.B])
                    O, Z = OZr.next()
                    def s_emit(kt2, SP, KT=KT, Q=Q):
                        for u in range(2):
                            kt = 2 * kt2 + u
                            ks = slice(kt * 128, (kt + 1) * 128)
                            mm(SP.a[:, u * TC:(u + 1) * TC], KT.a[0:64, ks], Q.a[0:64, :], True, True, [KT.B, Q.B], [SP.b[u]], u == 1)

                    def o_emit(kt2, P, V=V, O=O, Z=Z):
                        for u in range(2):
                            kt = 2 * kt2 + u
                            pu_ = P.a[:, u * TC:(u + 1) * TC]
                            mm(O.a[0:64, :], V.a[:, kt, 0:64], pu_, kt == 0, kt == NKT - 1, [V.B, P.B], [O.B], False)
                            mm(Z.a[0:64, :], ones[:, 0:64], pu_, kt == 0, kt == NKT - 1, [cm.B, P.B], [Z.B], u == 1)

                    attn_pipe(NKT // 2, SPr, Pr, s_emit, o_emit, 0.125)
                    r = r_r.next()
                    recip(r.a[0:64, :], Z.a[0:64, :], [Z.B], [r.B])
                    ob = ob_r.next()
                    tt("dve", ob.a[0:64, :], O.a[0:64, :], r.a[0:64, :], ALU.mult, [O.B, r.B], [ob.B])
                    S.dma("sp", mixT[512 + 64 * hq:576 + 64 * hq, cs], ob.a[0:64, :], reads=[ob.B])

    def phase_B_odd(i):
        S.barrier()
        arena.reset()
        j = i // 2
        KAr = Ring([arena.tile([SEQ], BF16) for _ in range(2)])
        QAr = Ring([arena.tile([NTOK], BF16) for _ in range(2)])
        Vr = Ring([arena.tile([NKT, 64], BF16) for _ in range(2)])
        Tr = Ring([arena.tile([8, TC], BF16, 16) for _ in range(2)])
        Pr = Ring([arena.tile([2 * TC], BF16) for _ in range(3)])
        SPr = Ring(PP[0:2])
        r_r = Ring([arena.tile([TC], F32) for _ in range(2)])
        ob_r = Ring([arena.tile([TC], BF16) for _ in range(2)])
        Sr = Ring(ps[0:4])
        OZr = Ring([(ps[4], ps[5]), (ps[6], ps[7])])
        for t_ in KAr.items:
            S.dma("pool", t_.a[64:80, :], I("naU"), writes=[t_.B])
        for t_ in QAr.items:
            S.dma("pool", t_.a[64:80, :], I("naV")[:, 0:NTOK], writes=[t_.B])
        for h in range(16):
            KA = KAr.next()
            S.dma("sp", KA.a[0:64, :], kT[64 * h:64 * h + 64, :], writes=[KA.B])
            QA = QAr.next()
            S.dma("sp", QA.a[0:64, :], qT[64 * h:64 * h + 64, :], writes=[QA.B])
            V = Vr.next()
            S.dma("sp", V.a, vtm[:, 64 * h:64 * h + 64].rearrange("(kt p) d -> p kt d", p=128), writes=[V.B])
            T = Tr.next()
            for a_ in range(2):
                for jj in range(8):
                    i0 = 15 - 2 * jj - a_
                    S.dma("pool", T.a[64 * a_:64 * a_ + 64, jj, :].rearrange("k (r q) -> k r q", q=64),
                          I("natab")[j, h, i0:i0 + 8].rearrange("r k q -> k r q"), writes=[T.b[a_ * 8 + jj]])
            for c in range(NCH):
                cs = slice(c * TC, (c + 1) * TC)
                jjs = [jj for jj in range(8) if 0 <= 4 * c - 2 + jj < NKT]
                O, Z = OZr.next()
                assert len(jjs) % 2 == 0
                def s_emit(n2, SP, KA=KA, QA=QA, T=T, jjs=jjs, c=c, cs=cs):
                    for u in range(2):
                        jj = jjs[2 * n2 + u]
                        gt = 4 * c - 2 + jj
                        ks = slice(gt * 128, (gt + 1) * 128)
                        so = SP.a[:, u * TC:(u + 1) * TC]
                        mm(so, KA.a[0:80, ks], QA.a[0:80, cs], True, False, [KA.B, QA.B], [SP.b[u]], False)
                        mm(so, ident, T.a[:, jj, :], False, True, [cm.B, T.b[jj], T.b[8 + jj]], [SP.b[u]], u == 1)

                def o_emit(n2, P, V=V, O=O, Z=Z, jjs=jjs, c=c):
                    for u in range(2):
                        n = 2 * n2 + u
                        gt = 4 * c - 2 + jjs[n]
                        pu_ = P.a[:, u * TC:(u + 1) * TC]
                        mm(O.a[0:64, :], V.a[:, gt, :], pu_, n == 0, n == len(jjs) - 1, [V.B, P.B], [O.B], False)
                        mm(Z.a[0:64, :], ones[:, 0:64], pu_, n == 0, n == len(jjs) - 1, [cm.B, P.B], [Z.B], u == 1)

                attn_pipe(len(jjs) // 2, SPr, Pr, s_emit, o_emit, 1.0)
                r = r_r.next()
                recip(r.a[0:64, :], Z.a[0:64, :], [Z.B], [r.B])
                ob = ob_r.next()
                tt("dve", ob.a[0:64, :], O.a[0:64, :], r.a[0:64, :], ALU.mult, [O.B, r.B], [ob.B])
                S.dma("sp", mixT[64 * h:64 * h + 64, cs], ob.a[0:64, :], reads=[ob.B])

    def phase_C(i):
        S.barrier()
        arena.reset()
        j = i // 2
        Wo = arena.tile([8, D], BF16, 8)
        load_w(Wo, (I("w_out_ab") if i % 2 == 0 else I("w_out_c"))[j], 8)
        xsr = Ring([arena.tile([8, TC], F32, 8) for _ in range(2)])
        mxr = Ring([arena.tile([8, TC], BF16) for _ in range(2)])
        hr = Ring([arena.tile([8, TC], BF16, 8) for _ in range(2)])
        scr = {"sq": arena.tile([8, TC], BF16, 8), "rt": arena.tile([TC], F32), "psring": Ring(ps[6:8])}
        psr = Ring(ps[0:6])
        for c in range(NCH):
            cs = slice(c * TC, (c + 1) * TC)
            xs = xsr.next()
            S.dma("sp", xs.a, xres[:, cs].rearrange("(k p) t -> p k t", p=128), writes=xs.b)
            mx = mxr.next()
            S.dma("sp", mx.a, mixT[:, cs].rearrange("(k p) t -> p k t", p=128), writes=[mx.B])
            for dc in range(8):
                ds = slice(dc * 128, (dc + 1) * 128)
                p_ = psr.next()
                for kc in range(8):
                    mm(p_.a, Wo.a[:, kc, ds], mx.a[:, kc, :], kc == 0, kc == 7, [Wo.b[kc], mx.B], [p_.B], kc == 7)
                tt("dve", xs.a[:, dc, :], p_.a, xs.a[:, dc, :], ALU.add, [p_.B, xs.b[dc]], [xs.b[dc]])
            S.dma("sp", xres[:, cs].rearrange("(k p) t -> p k t", p=128), xs.a, reads=xs.b)
            h2 = hr.next()
            rmsnorm(xs, "ffn_norm%d" % i, h2, scr)
            S.dma("sp", h2T[:, cs].rearrange("(k p) t -> p k t", p=128), h2.a, reads=h2.b)

    def phase_D(i):
        S.barrier()
        arena.reset()
        Wu = arena.tile([8, 2 * DFF], BF16, 8)
        Wd = arena.tile([NFC, D], BF16, NFC)
        load_w(Wu, I("w_ffn_up")[i], 8)
        load_w(Wd, I("w_ffn_down")[i], NFC)
        h2r = Ring([arena.tile([8, TC + 2], BF16) for _ in range(2)])
        g_t = arena.tile([NFC, TC], BF16, NFC)
        cg_r = Ring([arena.tile([256], F32) for _ in range(2)])
        cv_r = Ring([arena.tile([256], F32) for _ in range(2)])
        sg_r = Ring([arena.tile([256], F32) for _ in range(2)])
        xd_r = Ring([arena.tile([TC], F32) for _ in range(2)])
        psr = Ring(ps[0:8])
        cwo, _ = L.off["conv_w%d" % i]
        cbo, _ = L.off["conv_b%d" % i]

        def cw(jtap, m):
            return small.a[:, cwo + jtap * 44 + m:cwo + jtap * 44 + m + 1]

        def cb(m):
            return small.a[:, cbo + m:cbo + m + 1]

        for c in range(NCH):
            cs = slice(c * TC, (c + 1) * TC)
            h2 = h2r.next()
            lo = max(c * TC - 1, 0)
            hi = min(c * TC + TC + 1, NTOK)
            o0 = lo - (c * TC - 1)
            S.dma("sp", h2.a[:, :, o0:o0 + hi - lo], h2T[:, lo:hi].rearrange("(k p) t -> p k t", p=128), writes=[h2.B])
            if o0 > 0:
                S.op("dve", lambda e, o=h2.a[:, :, 0:1]: e.memset(o, 0.0), writes=[h2.B])
            if hi - lo + o0 < TC + 2:
                S.op("dve", lambda e, o=h2.a[:, :, TC + 1:TC + 2]: e.memset(o, 0.0), writes=[h2.B])
            for fc in range(NFC):
                for sub in range(2):
                    c0 = 256 * sub
                    res = []
                    for part, (ring_) in enumerate((cg_r, cv_r)):
                        m = part * NFC + fc
                        n0 = part * DFF + fc * 128
                        pu = psr.next()
                        for kc in range(8):
                            mm(pu.a[:, 0:258], Wu.a[:, kc, n0:n0 + 128], h2.a[:, kc, c0:c0 + 258], kc == 0, kc == 7,
                               [Wu.b[kc], h2.B], [pu.B], kc == 7)
                        cc = ring_.next()
                        act(cc.a, pu.a[:, 1:257], AF.Identity, [pu.B, small.B], [cc.B], bias=cb(m), scale=cw(1, m))
                        stt("dve", cc.a, pu.a[:, 0:256], cw(0, m), cc.a, ALU.mult, ALU.add, [pu.B, cc.B, small.B], [cc.B])
                        stt("dve", cc.a, pu.a[:, 2:258], cw(2, m), cc.a, ALU.mult, ALU.add, [pu.B, cc.B, small.B], [cc.B])
                        res.append(cc)
                    sg_ = sg_r.next()
                    act(sg_.a, res[0].a, AF.Silu, [res[0].B], [sg_.B])
                    tt("pool", g_t.a[:, fc, c0:c0 + 256], sg_.a, res[1].a, ALU.mult, [sg_.B, res[1].B], [g_t.b[fc]])
            for dc in range(8):
                ds = slice(dc * 128, (dc + 1) * 128)
                xd = xd_r.next()
                S.dma("sp", xd.a, xres[ds, cs], writes=[xd.B])
                p_ = psr.next()
                for fc in range(NFC):
                    mm(p_.a, Wd.a[:, fc, ds], g_t.a[:, fc, :], fc == 0, fc == NFC - 1, [Wd.b[fc], g_t.b[fc]], [p_.B], fc == NFC - 1)
                tt("dve", xd.a, p_.a, xd.a, ALU.add, [p_.B, xd.B], [xd.B])
                S.dma("sp", xres[ds, cs], xd.a, reads=[xd.B])

    first = True
    for i in layers:
        if stop == "P":
            break
        phase_EA(i, first)
        first = False
        if stop == "EA":
            break
        if i % 2 == 0:
            phase_B_even(i)
        else:
            phase_B_odd(i)
        if stop == "B":
            break
        phase_C(i)
        if dump and i == layers[-1]:
            S.barrier()
            for c in range(NCH):
                S.dma("sp", dbgT[:, c * TC:(c + 1) * TC], xres[:, c * TC:(c + 1) * TC])
        if stop == "C":
            break
        phase_D(i)
    if final and stop is None:
        phase_EA(layers[-1] + 1 if layers[-1] + 1 < DEPTH else DEPTH, False)
    if dump:
        S.barrier()
        for c in range(NCH):
            cs = slice(c * TC, (c + 1) * TC)
            S.dma("sp", outT[:, cs], xres[:, cs])
    S.barrier()
    S.emit()
    S.used_inputs = list(_ins.keys())
    return nc, S


_CACHE = {}


def host_consts(inp):
    U, Vm = na_rowmask()
    return {
        "small": pack_small(inp),
        "cmat": const_mats(),
        "ropeA": rope_tables()[0],
        "ropeB": rope_tables()[1],
        "natab": np.stack([na_tables(np.asarray(inp["c_rel_bias"][j])) for j in range(2)]),
        "naU": U,
        "naV": Vm,
    }


WNAMES = ("w_in_ab", "w_out_ab", "w_in_c", "w_out_c", "w_ffn_up", "w_ffn_down", "w_ple_gate", "w_ple_proj")


def kernel(**inp):
    inp = {k: np.asarray(v) for k, v in inp.items()}
    x = inp["x"]
    p = inp["p"]
    B = x.shape[0]
    if "nc" not in _CACHE:
        _CACHE["nc"] = build()[0]
    nc = _CACHE["nc"]
    consts = host_consts(inp)
    in_maps = []
    for b in range(B):
        m = dict(consts)
        m["xT"] = np.ascontiguousarray(x[b].T)
        m["pT"] = np.ascontiguousarray(np.transpose(p[:, b], (0, 2, 1)))
        for w in WNAMES:
            m[w] = np.ascontiguousarray(inp[w], dtype=np.float32)
        in_maps.append(m)
    res = run_bass_kernel_spmd(nc, in_maps, core_ids=list(range(B)))
    out = np.stack([np.ascontiguousarray(res.results[b]["outT"].T) for b in range(B)])
    return out.astype(np.float32)
```
